# Optimizing a Trainium2 kernel written in Bass

```python
import math
import jax, jax.numpy as jnp
from jax import lax
import numpy as np

D_MODEL = 1024
BATCH = 8
SEQ = 4096
DEPTH = 2

HEAD_DIM = 64
HEADS_A = 8
VDIM_A = 2 * HEAD_DIM
WIDTH_A = HEADS_A * VDIM_A
DILATED_GROUPS = ((128, 1), (512, 4), (2048, 16))
N_GROUPS_B = 3
HEADS_PER_GROUP_B = 4
HEADS_B = N_GROUPS_B * HEADS_PER_GROUP_B
WIDTH_B = HEADS_PER_GROUP_B * HEAD_DIM
D_FF = 4 * D_MODEL
ROPE_THETA = 500000.0
ROT_FRACTION = 4
Q_BLOCK = 128
EPS = 1e-6
NEG_INF = -1e30
Q_A_COLS = HEADS_A * 2 * HEAD_DIM
K_A_COLS = HEADS_A * 2 * HEAD_DIM
V_A_COLS = HEADS_A * VDIM_A
QKV_B_COLS = HEADS_B * HEAD_DIM
GATE_COLS = 2 * D_MODEL
IN_COLS = Q_A_COLS + K_A_COLS + V_A_COLS + 3 * QKV_B_COLS + GATE_COLS
IN_SPLITS = (
    Q_A_COLS,
    Q_A_COLS + K_A_COLS,
    Q_A_COLS + K_A_COLS + V_A_COLS,
    Q_A_COLS + K_A_COLS + V_A_COLS + QKV_B_COLS,
    Q_A_COLS + K_A_COLS + V_A_COLS + 2 * QKV_B_COLS,
    Q_A_COLS + K_A_COLS + V_A_COLS + 3 * QKV_B_COLS,
)

kernel_name = "hybrid_diffattn_dilated_encoder"


def rms_norm(x, g):
    xf = x.astype(jnp.float32)
    y = xf * lax.rsqrt(jnp.mean(xf * xf, axis=-1, keepdims=True) + EPS)
    return (y * g.astype(jnp.float32)).astype(x.dtype)


def rope_tables(positions):
    rot = HEAD_DIM // ROT_FRACTION
    inv_freq = ROPE_THETA ** (-jnp.arange(0, rot, 2, dtype=jnp.float32) / rot)
    ang = positions.astype(jnp.float32)[..., None] * inv_freq
    return jnp.cos(ang), jnp.sin(ang)


def apply_rope(x, cos, sin):
    half = cos.shape[-1]
    bshape = cos.shape[:2] + (1,) * (x.ndim - 3) + (half,)
    cs = cos.reshape(bshape).astype(x.dtype)
    sn = sin.reshape(bshape).astype(x.dtype)
    x1, x2, rest = x[..., :half], x[..., half:2 * half], x[..., 2 * half:]
    return jnp.concatenate([x1 * cs - x2 * sn, x2 * cs + x1 * sn, rest], axis=-1)


def diff_attention(q, k, v, lam):
    b, s, h, _, dh = q.shape
    nq = s // Q_BLOCK
    scale = dh ** -0.5
    qb = q.reshape(b, nq, Q_BLOCK, h, 2, dh).transpose(1, 0, 3, 4, 2, 5)
    kt = k.transpose(0, 2, 3, 1, 4)
    vt = v.transpose(0, 2, 1, 3)

    def block(qblk):
        sc = jnp.einsum('bhcqd,bhckd->bhcqk', qblk, kt).astype(jnp.float32) * scale
        p = jax.nn.softmax(sc, axis=-1)
        w = p[:, :, 0] - lam * p[:, :, 1]
        return jnp.einsum('bhqk,bhkd->bhqd', w.astype(vt.dtype), vt)

    o = lax.map(block, qb)
    return o.transpose(1, 0, 3, 2, 4).reshape(b, s, h, 2 * dh)


def dilated_window_attention(q, k, v, dilation, side):
    b, s, h, dh = q.shape
    L = s // dilation
    nb = -(-L // side)
    Lp = nb * side

    def split(t):
        return t.reshape(b, L, dilation, h, dh).transpose(0, 2, 3, 1, 4)

    qs = jnp.pad(split(q), ((0, 0),) * 3 + ((0, Lp - L), (0, 0)))
    pad_kv = ((0, 0),) * 3 + ((side, Lp - L + side), (0, 0))
    kp = jnp.pad(split(k), pad_kv)
    vp = jnp.pad(split(v), pad_kv)

    def band(t):
        return jnp.concatenate(
            [t[..., j * side:j * side + Lp, :].reshape(b, dilation, h, nb, side, dh) for j in range(3)],
            axis=-2)

    kb, vb = band(kp), band(vp)
    qb = qs.reshape(b, dilation, h, nb, side, dh)
    sc = jnp.einsum('bghnqd,bghnkd->bghnqk', qb, kb).astype(jnp.float32) * (dh ** -0.5)
    blk = jnp.arange(nb)[:, None, None]
    qpos = blk * side + jnp.arange(side)[None, :, None]
    kpos = blk * side - side + jnp.arange(3 * side)[None, None, :]
    valid = (jnp.abs(qpos - kpos) <= side) & (kpos >= 0) & (kpos < L)
    sc = jnp.where(valid, sc, NEG_INF)
    lse = jax.nn.logsumexp(sc, axis=-1)
    p = jnp.exp(sc - lse[..., None])
    o = jnp.einsum('bghnqk,bghnkd->bghnqd', p.astype(v.dtype), vb)
    o = o.reshape(b, dilation, h, Lp, dh)[..., :L, :].transpose(0, 3, 1, 2, 4).reshape(b, s, h, dh)
    lse = lse.reshape(b, dilation, h, Lp)[..., :L].transpose(0, 3, 1, 2).reshape(b, s, h)
    return o, lse


def setup_inputs(seed: int = 0) -> dict:
    key = jax.random.key(seed)
    ks = jax.random.split(key, 20)
    D = D_MODEL
    nrm = lambda k, shape, s: jax.random.normal(k, shape, jnp.float32) * s
    x = nrm(ks[0], (BATCH, SEQ, D), 1.0)
    c = nrm(ks[1], (BATCH, D), 1.0)
    offsets = jax.random.randint(ks[2], (BATCH, 1), 0, 1024, dtype=jnp.int32)
    positions = (jnp.arange(SEQ, dtype=jnp.int32)[None, :] + offsets).astype(jnp.int32)
    return {
        "x": x,
        "c": c,
        "positions": positions,
        "ada_w": nrm(ks[3], (DEPTH, D, 6 * D), D ** -0.5),
        "ada_b": nrm(ks[4], (DEPTH, 6 * D), 0.02),
        "norm_mix_g": 1.0 + nrm(ks[5], (DEPTH, D), 0.02),
        "norm_mlp_g": 1.0 + nrm(ks[6], (DEPTH, D), 0.02),
        "w_in": nrm(ks[7], (DEPTH, D, IN_COLS), D ** -0.5),
        "qk_gain_a": 1.0 + nrm(ks[8], (DEPTH, 2, HEAD_DIM), 0.02),
        "lambda_a": nrm(ks[9], (DEPTH, 4, HEAD_DIM), 0.1),
        "subln_g_a": 1.0 + nrm(ks[10], (DEPTH, VDIM_A), 0.02),
        "qk_gain_b": 1.0 + nrm(ks[11], (DEPTH, 2, HEAD_DIM), 0.02),
        "w_branch_a": nrm(ks[12], (DEPTH, WIDTH_A, D), WIDTH_A ** -0.5),
        "w_branch_b": nrm(ks[13], (DEPTH, WIDTH_B, D), WIDTH_B ** -0.5),
        "gate_bias": nrm(ks[14], (DEPTH, GATE_COLS), 0.02),
        "w_out": nrm(ks[15], (DEPTH, D, D), D ** -0.5),
        "w_mlp_up": nrm(ks[16], (DEPTH, D, D_FF), D ** -0.5),
        "w_mlp_down": nrm(ks[17], (DEPTH, D_FF, D), D_FF ** -0.5),
    }


def reference(x, c, positions, ada_w, ada_b, norm_mix_g, norm_mlp_g, w_in, qk_gain_a,
              lambda_a, subln_g_a, qk_gain_b, w_branch_a, w_branch_b, gate_bias, w_out,
              w_mlp_up, w_mlp_down):
    b, s, _ = x.shape
    cos, sin = rope_tables(positions)
    cond = jax.nn.silu(c)
    for l in range(DEPTH):
        mod = (cond @ ada_w[l] + ada_b[l])[:, None, :]
        shift_m, scale_m, gate_m, shift_f, scale_f, gate_f = jnp.split(mod, 6, axis=-1)

        h = rms_norm(x, norm_mix_g[l]) * (1.0 + scale_m) + shift_m
        proj = h @ w_in[l]
        qa, ka, va, qb, kb, vb, gates = jnp.split(proj, IN_SPLITS, axis=-1)

        qa = apply_rope(rms_norm(qa.reshape(b, s, HEADS_A, 2, HEAD_DIM), qk_gain_a[l, 0]), cos, sin)
        ka = apply_rope(rms_norm(ka.reshape(b, s, HEADS_A, 2, HEAD_DIM), qk_gain_a[l, 1]), cos, sin)
        va = va.reshape(b, s, HEADS_A, VDIM_A)
        lam_init = 0.8 - 0.6 * math.exp(-0.3 * l)
        lv = lambda_a[l].astype(jnp.float32)
        lam = jnp.exp(jnp.sum(lv[0] * lv[1])) - jnp.exp(jnp.sum(lv[2] * lv[3])) + lam_init
        oa = diff_attention(qa, ka, va, lam)
        oa = (rms_norm(oa, subln_g_a[l]) * (1.0 - lam_init)).reshape(b, s, WIDTH_A)

        qb = apply_rope(rms_norm(qb.reshape(b, s, N_GROUPS_B, HEADS_PER_GROUP_B, HEAD_DIM), qk_gain_b[l, 0]), cos, sin)
        kb = apply_rope(rms_norm(kb.reshape(b, s, N_GROUPS_B, HEADS_PER_GROUP_B, HEAD_DIM), qk_gain_b[l, 1]), cos, sin)
        vb = vb.reshape(b, s, N_GROUPS_B, HEADS_PER_GROUP_B, HEAD_DIM)
        outs, lses = [], []
        for g, (window, dilation) in enumerate(DILATED_GROUPS):
            o_g, lse_g = dilated_window_attention(qb[:, :, g], kb[:, :, g], vb[:, :, g],
                                                  dilation, window // (2 * dilation))
            outs.append(o_g)
            lses.append(lse_g)
        wgt = jax.nn.softmax(jnp.stack(lses, axis=0), axis=0)
        ob = jnp.sum(wgt[..., None].astype(x.dtype) * jnp.stack(outs, axis=0), axis=0)
        ob = ob.reshape(b, s, WIDTH_B)

        g_a, g_b = jnp.split(jax.nn.sigmoid(gates + gate_bias[l]), 2, axis=-1)
        y = (g_a * (oa @ w_branch_a[l]) + g_b * (ob @ w_branch_b[l])) @ w_out[l]
        x = x + gate_m * y

        h = rms_norm(x, norm_mlp_g[l]) * (1.0 + scale_f) + shift_f
        x = x + gate_f * (jnp.square(jax.nn.relu(h @ w_mlp_up[l])) @ w_mlp_down[l])
    return x
```

```python
import math
import os
import numpy as np
import ml_dtypes
import concourse.bass as bass
import concourse.mybir as mybir
from concourse.bass_utils import run_bass_kernel_spmd

F32 = mybir.dt.float32
BF16 = mybir.dt.bfloat16
I32 = mybir.dt.int32
AF = mybir.ActivationFunctionType
ALU = mybir.AluOpType

D = 1024
S = 4096
DEPTH = 2
DFF = 4096
INC = 7424
NB = 8
TB = 512
EPS = 1e-6
PI = math.pi
LW = 96
PBASE = 12
NPAR = PBASE + DEPTH * LW
GROUPS = ((128, 1), (512, 4), (2048, 16))


class Buf:
    __slots__ = ("name", "w", "r")

    def __init__(self, prog, name):
        self.name = name
        self.w = None
        self.r = []
        prog.bufs.append(self)


class Op:
    __slots__ = ("eng", "fn", "deps", "needs_inc", "tok", "is_dma")

    def __init__(self, eng, fn, is_dma=False):
        self.eng = eng
        self.fn = fn
        self.deps = []
        self.needs_inc = False
        self.tok = None
        self.is_dma = is_dma


ENGS = ("pe", "act", "dve", "pool", "sp")


class Prog:
    def __init__(self, nc):
        self.nc = nc
        self.ops = {e: [] for e in ENGS}
        self.bufs = []
        self.last = {e: None for e in ENGS}
        self.pending = {e: [] for e in ENGS}
        self.dma_keys = {}
        self.esem = {}

    def buf(self, name):
        return Buf(self, name)

    def _record(self, o, reads, writes):
        deps = []
        for b in reads:
            if b.w is not None:
                deps.append(b.w)
        for b in writes:
            if b.w is not None:
                deps.append(b.w)
            deps.extend(b.r)
        deps.extend(self.pending[o.eng])
        self.pending[o.eng] = []
        seen = set()
        for d in deps:
            if d is o or id(d) in seen:
                continue
            seen.add(id(d))
            if d.eng == "pe" and o.eng == "pe" and not d.is_dma and not o.is_dma:
                continue
            o.deps.append(d)
        for b in reads:
            b.r.append(o)
        for b in writes:
            b.w = o
            b.r = []
        self.ops[o.eng].append(o)
        if not o.is_dma:
            self.last[o.eng] = o
        return o

    def op(self, eng, fn, reads=(), writes=()):
        return self._record(Op(eng, fn), reads, writes)

    def dma(self, eng, key, out, in_, reads=(), writes=()):
        if key not in self.dma_keys:
            self.dma_keys[key] = [self.nc.semaphore("d_" + key).__enter__(), 0, None]
        ent = self.dma_keys[key]
        ent[1] += 1
        o = Op(eng, lambda e, out=out, in_=in_: e.dma_start(out=out, in_=in_), is_dma=True)
        o.tok = (ent[0], 16 * ent[1])
        self._record(o, reads, writes)
        o.deps = [d for d in o.deps if not (d.is_dma and d.tok[0] is ent[0] and d.eng == eng)]
        ent[2] = o
        return o

    def barrier(self):
        tails = [self.last[e] for e in ENGS if self.last[e] is not None]
        tails += [ent[2] for ent in self.dma_keys.values() if ent[2] is not None]
        for e in ENGS:
            self.pending[e] = list(tails)
        for b in self.bufs:
            b.w = None
            b.r = []

    def emit(self):
        nc = self.nc
        for e in ENGS:
            for o in self.ops[e]:
                for d in o.deps:
                    if not d.is_dma:
                        d.needs_inc = True
        for e in ENGS:
            if e not in self.esem:
                self.esem[e] = nc.semaphore("e_" + e).__enter__()
            cnt = 0
            for o in self.ops[e]:
                if not o.is_dma and o.needs_inc:
                    cnt += 1
                    o.tok = (self.esem[e], cnt)
        final_waits = [(ent[0], 16 * ent[1]) for ent in self.dma_keys.values()]

        def run(ename, eng):
            waited = {}
            for o in self.ops[ename]:
                need = {}
                for d in o.deps:
                    sem, val = d.tok
                    if need.get(id(sem), (None, 0))[1] < val:
                        need[id(sem)] = (sem, val)
                for sid, (sem, val) in need.items():
                    if waited.get(sid, 0) < val:
                        eng.wait_ge(sem, val)
                        waited[sid] = val
                ins = o.fn(eng)
                if o.is_dma:
                    ins.then_inc(o.tok[0], 16)
                elif o.needs_inc:
                    ins.then_inc(o.tok[0], 1)
            if ename == "sp":
                for sem, val in final_waits:
                    if waited.get(id(sem), 0) < val:
                        eng.wait_ge(sem, val)

        with nc.Block() as block:
            @block.tensor
            def _(eng):
                run("pe", eng)

            @block.scalar
            def _(eng):
                run("act", eng)

            @block.vector
            def _(eng):
                run("dve", eng)

            @block.gpsimd
            def _(eng):
                run("pool", eng)

            @block.sync
            def _(eng):
                run("sp", eng)


class Arena:
    def __init__(self, t16, nbytes):
        self.t16 = t16
        self.t32 = t16.bitcast(F32)
        self.ti32 = t16.bitcast(I32)
        self.nbytes = nbytes
        self.off = 0

    def reset(self):
        self.off = 0

    def take(self, nbytes, dtype):
        rb = (nbytes + 63) // 64 * 64
        o = self.off
        self.off += rb
        assert self.off <= self.nbytes, ("arena overflow", self.off, self.nbytes)
        if dtype == BF16:
            return self.t16[:, o // 2:(o + nbytes) // 2]
        if dtype == I32:
            return self.ti32[:, o // 4:(o + nbytes) // 4]
        return self.t32[:, o // 4:(o + nbytes) // 4]


def build(dbg=None):
    nc = bass.Bass("TRN2", target_bir_lowering=False)
    P = Prog(nc)
    dram = {}

    def din(name, shape, dt):
        dram[name] = nc.dram_tensor(name, list(shape), dt, kind="ExternalInput")
        return dram[name]

    x_d = din("x", [S, D], F32)
    pos_d = din("pos", [1, S], I32)
    par_d = din("params", [128, NPAR], F32)
    idf_d = din("identf", [128, 128], F32)
    cm_d = din("cm16", [128, 384], BF16)
    mk_d = din("masks", [128, 512], BF16)
    adaw_d = din("ada_w", [DEPTH, D, 6 * D], F32)
    win_d = din("w_in", [DEPTH, D, INC], F32)
    wa_d = din("w_branch_a", [DEPTH, D, D], F32)
    wb_d = din("w_branch_b", [DEPTH, 256, D], F32)
    wo_d = din("w_out", [DEPTH, D, D], F32)
    wu_d = din("w_mlp_up", [DEPTH, D, DFF], F32)
    wd_d = din("w_mlp_down", [DEPTH, DFF, D], F32)
    out_d = nc.dram_tensor("out", [S, D], F32, kind="ExternalOutput")
    xs_d = nc.dram_tensor("xs_scratch", [S, D], F32)
    oc_d = nc.dram_tensor("ocat_scratch", [1280, S], BF16)
    wg16 = nc.dram_tensor("wg16", [128, 16, 8, 128], BF16)
    ada16 = nc.dram_tensor("ada16", [128, 12, 8, 512], BF16)
    wa16 = nc.dram_tensor("wa16", [128, 8, 8, 128], BF16)
    wb16 = nc.dram_tensor("wb16", [128, 8, 2, 128], BF16)
    wo16 = nc.dram_tensor("wo16", [128, 2, 8, 512], BF16)
    wu16 = nc.dram_tensor("wu16", [128, 32, 8, 128], BF16)
    wd16 = nc.dram_tensor("wd16", [128, 2, 32, 512], BF16)
    dbg_out = {}

    def dbg_tensor(name, shape, dt):
        dbg_out[name] = nc.dram_tensor(name, list(shape), dt, kind="ExternalOutput")
        return dbg_out[name]

    sb = nc.alloc_sbuf_tensor
    hT = sb("hT", [128, 8, S], BF16)
    Ctab = sb("Ctab", [128, S], F32)
    Stab = sb("Stab", [128, S], F32)
    identf = sb("identf_sb", [128, 128], F32)
    onesf = sb("onesf", [128, 128], F32)
    cm16 = sb("cm16_sb", [128, 384], BF16)
    masks = sb("masks_sb", [128, 512], BF16)
    par = sb("par_sb", [128, NPAR], F32)
    cond16 = sb("cond16", [128, 8], BF16)
    modT = sb("modT", [128, 48], F32)
    lv = sb("lvec", [128, 64], F32)
    gm_bc = sb("gm_bc", [128, D], F32)
    gf_bc = sb("gf_bc", [128, D], F32)
    small = sb("small", [128, 64], F32)
    psb = [nc.alloc_psum_tensor("psb%d" % i, [128, 512], F32) for i in range(8)]
    pB = [P.buf("psum%d" % i) for i in range(8)]
    arena_bytes = (nc.sbuf_bytes_remaining - 1024) // 64 * 64
    arena_t = sb("arena", [128, arena_bytes // 2], BF16)
    AR = Arena(arena_t, arena_bytes)

    onesblk = cm16[:, 0:128]
    perm = cm16[:, 128:256]
    ones16 = cm16[:, 256:384]

    bHT = [P.buf("hT%d" % b) for b in range(NB)]
    bC = P.buf("Ctab")
    bS = P.buf("Stab")
    bconst = P.buf("consts")
    bmod = P.buf("modT")
    blv = P.buf("lv")
    bgm = P.buf("gm_bc")
    bgf = P.buf("gf_bc")
    bsmall = P.buf("small")
    bxs = [P.buf("xs%d" % b) for b in range(NB)]
    boc = [P.buf("oc%d" % b) for b in range(NB)]

    def mm(out, lhsT, rhs, start, stop, reads, writes):
        P.op("pe", lambda e: e.matmul(out, lhsT, rhs, start=start, stop=stop), reads, writes)

    def tr(out, in_, reads, writes):
        P.op("pe", lambda e: e.transpose(out, in_, identf[:, :]), reads, writes)

    def act(out, in_, func, reads, writes, bias=0.0, scale=1.0, accum_out=None):
        if accum_out is None:
            P.op("act", lambda e: e.activation(out, in_, func, bias=bias, scale=scale), reads, writes)
        else:
            P.op("act", lambda e: e.activation(out, in_, func, bias=bias, scale=scale,
                                                accum_out=accum_out), reads, writes)

    def ts(out, in0, s1, s2, op0, op1, reads, writes, eng="dve"):
        if s2 is None:
            P.op(eng, lambda e: e.tensor_scalar(out, in0, s1, None, op0), reads, writes)
        else:
            P.op(eng, lambda e: e.tensor_scalar(out, in0, s1, s2, op0, op1), reads, writes)

    def tt(out, in0, in1, op, reads, writes, eng="dve"):
        P.op(eng, lambda e: e.tensor_tensor(out, in0, in1, op), reads, writes)

    def stt(out, in0, scalar, in1, op0, op1, reads, writes):
        P.op("dve", lambda e: e.scalar_tensor_tensor(out, in0, scalar, in1, op0, op1), reads, writes)

    def cp(out, in_, reads, writes, eng="dve"):
        P.op(eng, lambda e: e.tensor_copy(out, in_), reads, writes)

    def recip(out, in_, reads, writes):
        P.op("dve", lambda e: e.reciprocal(out, in_), reads, writes)

    def memset(ap, val, writes, eng="dve"):
        P.op(eng, lambda e: e.memset(ap, val), (), writes)

    def wslab(l_w, rows0, nk, c0, nc_):
        return l_w[rows0:rows0 + nk * 128, c0:c0 + nc_].rearrange("(kc p) n -> p kc n", p=128)

    def dump(name, src_ap, shape, dt, reads):
        t = dbg_tensor(name, shape, dt)
        P.dma("sp", "dbg_" + name, t.ap(), src_ap, reads=reads, writes=())

    AR.reset()
    P.dma("sp", "c_par", par[:, :], par_d.ap(), (), [bconst])
    P.dma("sp", "c_idf", identf[:, :], idf_d.ap(), (), [bconst])
    P.dma("sp", "c_cm", cm16[:, :], cm_d.ap(), (), [bconst])
    P.dma("sp", "c_mk", masks[:, :], mk_d.ap(), (), [bconst])
    memset(onesf[:, :], 1.0, [bconst])
    act(cond16[:, :], par[:, 0:8], AF.Silu, [bconst], [bconst])

    invf = par[:, 8:9]
    a_c = par[:, 9:10]
    b_c = par[:, 10:11]
    a_s = par[:, 11:12]
    posi = [AR.take(TB * 4, I32) for _ in range(2)]
    bposi = [P.buf("posi%d" % i) for i in range(2)]
    rt = [AR.take(TB * 4, F32) for _ in range(6)]
    brt = [P.buf("rt%d" % i) for i in range(6)]
    C1 = 6.28125
    C2 = 2.0 * PI - C1
    def rope_block(b):
        cs = slice(b * TB, (b + 1) * TB)
        pi_, bpi = posi[b % 2], bposi[b % 2]
        src = bass.AP(pos_d, b * TB, [[0, 128], [1, TB]])
        P.dma("sp", "posi%d" % (b % 2), pi_, src, (), [bpi])
        ang, kf, r, m, rc, so = rt
        bang, bkf, br_, bm, brc, bso = brt
        cp(ang, pi_, [bpi], [bang])
        ts(ang, ang, invf, None, ALU.mult, None, [bang, bconst], [bang])
        ki = pi_
        ts(ki, ang, 1.0 / (2.0 * PI), None, ALU.mult, None, [bang], [bpi])
        cp(kf, ki, [bpi], [bkf])
        stt(r, kf, -C1, ang, ALU.mult, ALU.add, [bkf, bang], [br_])
        stt(r, kf, -C2, r, ALU.mult, ALU.add, [bkf, br_], [br_])
        ts(m, r, PI, 2.0 * PI, ALU.is_gt, ALU.mult, [br_], [bm])
        tt(r, r, m, ALU.subtract, [br_, bm], [br_])
        ts(m, r, -PI, 2.0 * PI, ALU.is_lt, ALU.mult, [br_], [bm])
        tt(r, r, m, ALU.add, [br_, bm], [br_])
        ts(rc, r, PI / 2.0, None, ALU.add, None, [br_], [brc])
        ts(m, rc, PI, 2.0 * PI, ALU.is_gt, ALU.mult, [brc], [bm])
        tt(rc, rc, m, ALU.subtract, [brc, bm], [brc])
        ts(r, r, PI, -PI, ALU.min, ALU.max, [br_], [br_])
        ts(rc, rc, PI, -PI, ALU.min, ALU.max, [brc], [brc])
        act(so, r, AF.Sin, [br_], [bso])
        ts(Stab[:, cs], so, a_s, None, ALU.mult, None, [bso, bconst], [bS])
        act(so, rc, AF.Sin, [brc], [bso])
        ts(Ctab[:, cs], so, a_c, b_c, ALU.mult, ALU.add, [bso, bconst], [bC])


    def ada(l):
        base = PBASE + l * LW
        if l == 0:
            wbuf = [AR.take(8 * 512 * 2, BF16).rearrange("p (k n) -> p k n", n=512) for _ in range(2)]
        else:
            P.barrier()
            AR.reset()
            wbuf = [AR.take(8 * 512 * 2, BF16).rearrange("p (k n) -> p k n", n=512) for _ in range(2)]
        bw = [P.buf("adaw%d_%d" % (l, i)) for i in range(2)]
        aw = adaw_d.ap()[l]
        for cc in range(12):
            w, bwb = wbuf[cc % 2], bw[cc % 2]
            if l == 0:
                P.dma("pool", "adaw%d" % (cc % 2), w, wslab(aw, 0, 8, cc * 512, 512), (), [bwb])
            else:
                P.dma("sp", "adaw%d" % (cc % 2), w, ada16.ap()[:, cc], (), [bwb])
            pb = pB[cc % 2]
            ps = psb[cc % 2]
            for f in range(4):
                for kc in range(8):
                    mm(ps[:, f:f + 1], w[:, kc, f * 128:(f + 1) * 128], cond16[:, kc:kc + 1],
                       kc == 0, kc == 7, [bwb, bconst], [pb])
            for f in range(4):
                col = cc * 4 + f
                act(modT[:, col:col + 1], ps[:, f:f + 1], AF.Identity, [pb, bconst], [bmod],
                    bias=par[:, base + 16 + col:base + 17 + col])
            if l == 0 and cc < NB:
                rope_block(cc)
        stt(lv[:, 0:8], modT[:, 8:16], 1.0, par[:, base:base + 8], ALU.add, ALU.mult, [bmod, bconst], [blv])
        ts(lv[:, 0:8], lv[:, 0:8], 32.0, None, ALU.mult, None, [blv], [blv])
        stt(lv[:, 8:16], modT[:, 32:40], 1.0, par[:, base + 8:base + 16], ALU.add, ALU.mult, [bmod, bconst], [blv])
        ts(lv[:, 8:16], lv[:, 8:16], 32.0, None, ALU.mult, None, [blv], [blv])
        G = [AR.take(128 * 4, F32) for _ in range(2)]
        bG = [P.buf("G%d" % i) for i in range(2)]
        for gi, (c0, dst, bdst) in enumerate(((16, gm_bc, bgm), (40, gf_bc, bgf))):
            for f in range(8):
                g, bg = G[f % 2], bG[f % 2]
                ts(g, onesf[:, :], modT[:, c0 + f:c0 + f + 1], None, ALU.mult, None, [bmod, bconst], [bg])
                pbk = pB[2 + (f // 4) + 2 * gi]
                tr(psb[2 + (f // 4) + 2 * gi][:, (f % 4) * 128:(f % 4 + 1) * 128], g, [bg, bconst], [pbk])
            for hh in range(2):
                cp(dst[:, hh * 512:(hh + 1) * 512], psb[2 + hh + 2 * gi][:, :], [pB[2 + hh + 2 * gi]], [bdst])
        lam_init = 0.8 - 0.6 * math.exp(-0.3 * l)
        lamb = base + 85
        tt(small[0:64, 0:1], par[0:64, lamb:lamb + 1], par[0:64, lamb + 1:lamb + 2], ALU.mult, [bconst], [bsmall])
        tt(small[0:64, 1:2], par[0:64, lamb + 2:lamb + 3], par[0:64, lamb + 3:lamb + 4], ALU.mult, [bconst, bsmall], [bsmall])
        mm(psb[6][:, 0:2], onesf[0:64, :], small[0:64, 0:2], True, True, [bsmall, bconst], [pB[6]])
        act(small[:, 2:4], psb[6][:, 0:2], AF.Exp, [pB[6]], [bsmall])
        tt(lv[:, 16:17], small[:, 3:4], small[:, 2:3], ALU.subtract, [bsmall], [blv])
        ts(lv[:, 16:17], lv[:, 16:17], -lam_init, None, ALU.add, None, [blv], [blv])
        ts(lv[:, 17:18], par[:, base + 84:base + 85], (1.0 - lam_init) * math.sqrt(128.0), None,
           ALU.mult, None, [bconst], [blv])

    def stage_norm(xt, bxt, geff_col, shift_ap, dstT, bdst, xn, bxn, junk, bjunk, ss, bss, pbanks):
        for i in range(4):
            act(junk, xt[i], AF.Square, [bxt], [bjunk, bss], accum_out=ss[:, i:i + 1])
        act(ss[:, 4:8], ss[:, 0:4], AF.Ln, [bss], [bss], bias=float(D * EPS))
        act(ss[:, 8:12], ss[:, 4:8], AF.Exp, [bss], [bss], scale=-0.5)
        for i in range(4):
            ts(xn[i], xt[i], ss[:, 8 + i:9 + i], None, ALU.mult, None, [bxt, bss], [bxn[i]])
        for f in range(8):
            bk = pbanks[f % len(pbanks)]
            for i in range(4):
                tr(psb[bk][:, i * 128:(i + 1) * 128], xn[i][:, f * 128:(f + 1) * 128], [bxn[i], bconst], [pB[bk]])
            act(dstT(f), psb[bk][:, :], AF.Identity, [pB[bk], blv, bmod], [bdst],
                bias=shift_ap(f), scale=lv[:, geff_col + f:geff_col + f + 1])

    def stage_a0(l):
        P.barrier()
        AR.reset()
        xts = [[AR.take(D * 4, F32) for _ in range(4)] for _ in range(2)]
        bxts = [P.buf("xt%d" % i) for i in range(2)]
        xn = [AR.take(D * 4, F32) for _ in range(4)]
        bxn = [P.buf("xn%d" % i) for i in range(4)]
        junk = AR.take(D * 2, BF16)
        bjunk = P.buf("junk")
        ss = AR.take(64, F32)
        bss = P.buf("ss")
        for b in range(NB):
            xt, bxt = xts[b % 2], bxts[b % 2]
            for i in range(4):
                r0 = b * TB + i * 128
                P.dma("sp", "xt%d" % (b % 2), xt[i], x_d.ap()[r0:r0 + 128, :], (), [bxt])
            stage_norm(xt, bxt, 0, lambda f: modT[:, f:f + 1],
                       lambda f, b=b: hT[:, f, b * TB:(b + 1) * TB], bHT[b],
                       xn, bxn, junk, bjunk, ss, bss, [0, 1, 2, 3])

    def qk_pipeline(l, wt, bwt, gain_col, b, nrows, writer, T, bT, banks):
        sq16, qg, qg16, lnv, rstd, t1, t2 = T
        bsq, bqg, bqg16, bln, brs, bt1, bt2 = bT
        pq, pss, psw = banks
        cs = slice(b * TB, (b + 1) * TB)
        R = slice(0, nrows)
        for kc in range(8):
            mm(psb[pq][R, :], wt[:, kc, 0:nrows], hT[:, kc, cs], kc == 0, kc == 7, [bwt, bHT[b]], [pB[pq]])
        act(sq16[R, :], psb[pq][R, :], AF.Square, [pB[pq]], [bsq])
        act(qg[R, :], psb[pq][R, :], AF.Identity, [pB[pq], bconst], [bqg], scale=par[R, gain_col:gain_col + 1])
        cp(qg16[R, :], qg[R, :], [bqg], [bqg16])
        mm(psb[pss][R, :], onesblk[R, 0:nrows], sq16[R, :], True, True, [bsq, bconst], [pB[pss]])
        mm(psb[psw][R, :], perm[R, 0:nrows], qg16[R, :], True, True, [bqg16, bconst], [pB[psw]])
        act(lnv[R, :], psb[pss][R, :], AF.Ln, [pB[pss]], [bln], bias=float(64 * EPS))
        act(rstd[R, :], lnv[R, :], AF.Exp, [bln], [brs], scale=-0.5)
        tt(t1[R, :], qg[R, :], Ctab[R, cs], ALU.mult, [bqg, bC], [bt1], eng="pool")
        tt(t2[R, :], psb[psw][R, :], Stab[R, cs], ALU.mult, [pB[psw], bS], [bt2])
        tt(t1[R, :], t1[R, :], t2[R, :], ALU.add, [bt1, bt2], [bt1])
        writer(t1, rstd, [bt1, brs])

    def phase_b1(l):
        P.barrier()
        AR.reset()
        cvt = convert_weights(l)
        base = PBASE + l * LW
        wl = win_d.ap()[l]
        QT1 = AR.take(S * 2, BF16)
        QT2 = AR.take(S * 2, BF16)
        KT = AR.take(S * 2, BF16)
        VA = AR.take(S * 2, BF16).rearrange("p (t d) -> p t d", d=128)
        bQ = [P.buf("QT_%d" % b) for b in range(NB)]
        bK = P.buf("KT")
        bV = P.buf("VA")
        memset(QT1, 0.0, bQ)
        memset(QT2, 0.0, bQ)
        wq, wk, wv = [AR.take(8 * 128 * 2, BF16).rearrange("p (k n) -> p k n", n=128) for _ in range(3)]
        bwq, bwk, bwv = P.buf("wq"), P.buf("wk"), P.buf("wv")
        T = [AR.take(TB * 2, BF16), AR.take(TB * 4, F32), AR.take(TB * 2, BF16), AR.take(TB * 4, F32),
             AR.take(TB * 4, F32), AR.take(TB * 4, F32), AR.take(TB * 4, F32)]
        bT = [P.buf("T%d" % i) for i in range(7)]
        NPB = 6
        pbuf = [AR.take(TB * 2, BF16) for _ in range(NPB)]
        NQS = 4
        qs_ = [AR.take(TB * 2, BF16) for _ in range(NQS)]
        bqs = [P.buf("qs%d" % i) for i in range(NQS)]
        bp = [P.buf("pbuf%d" % i) for i in range(NPB)]
        rinv = AR.take(TB * 4, F32)
        brinv = P.buf("rinv")
        oc = [AR.take(TB * 4, F32) for _ in range(2)]
        boc2 = [P.buf("ocx%d" % i) for i in range(2)]
        oa = AR.take(TB * 4, F32)
        boa = P.buf("oa")
        sqb = AR.take(TB * 2, BF16)
        bsqb = P.buf("sqb")
        lnb = AR.take(TB * 4, F32)
        blnb = P.buf("lnb")
        rsb = AR.take(TB * 4, F32)
        brsb = P.buf("rsb")
        o16 = [AR.take(TB * 2, BF16) for _ in range(2)]
        bo16 = [P.buf("o16_%d" % i) for i in range(2)]
        scnt = 0
        ucnt = 0
        for h in range(8):
            P.dma("pool", "wq", wq, wslab(wl, 0, 8, h * 128, 128), (), [bwq])
            P.dma("pool", "wk", wk, wslab(wl, 0, 8, 1024 + h * 128, 128), (), [bwk])
            P.dma("pool", "wv", wv, wslab(wl, 0, 8, 2048 + h * 128, 128), (), [bwv])
            ncv = (len(cvt) + 7 - h) // (8 - h) if dbg not in ("qk", "att") else len(cvt)
            for _ in range(ncv):
                cvt.pop(0)()
            for b in range(NB):
                cs = slice(b * TB, (b + 1) * TB)

                def wr_q(t1, rstd, reads, cs=cs, b=b):
                    tt(QT1[0:64, cs], t1[0:64, :], rstd[0:64, :], ALU.mult, reads, [bQ[b]])
                    tt(QT2[64:128, cs], t1[64:128, :], rstd[64:128, :], ALU.mult, reads, [bQ[b]])

                def wr_k(t1, rstd, reads, cs=cs):
                    tt(KT[:, cs], t1[:, :], rstd[:, :], ALU.mult, reads, [bK])

                qk_pipeline(l, wq, bwq, base + 80, b, 128, wr_q, T, bT, (7, 0, 1))
                qk_pipeline(l, wk, bwk, base + 81, b, 128, wr_k, T, bT, (2, 0, 1))
            for t4 in range(8):
                bk = 3 + (t4 % 2)
                for i in range(4):
                    tk = t4 * 4 + i
                    for kc in range(8):
                        mm(psb[bk][:, i * 128:(i + 1) * 128], hT[:, kc, tk * 128:(tk + 1) * 128], wv[:, kc, :],
                           kc == 0, kc == 7, [bwv, bHT[tk // 4]], [pB[bk]])
                cp(VA[:, t4 * 4:(t4 + 1) * 4, :], psb[bk][:, :].rearrange("p (t d) -> p t d", d=128),
                   [pB[bk]], [bV])
            if dbg in ("qk", "att") and h == 0 and l == 0:
                dump("d_QT1", QT1, [128, S], BF16, bQ)
                dump("d_QT2", QT2, [128, S], BF16, bQ)
                dump("d_KT", KT, [128, S], BF16, [bK])
                dump("d_VA", VA, [128, 32, 128], BF16, [bV])
            iters = [(qb, c, kt) for qb in range(NB) for c in range(2) for kt in range(32)]
            nit = len(iters)
            LOOK = 2
            DSUM = 4
            deferred = []
            dseq = [0]

            def defer(due, fn):
                dseq[0] += 1
                deferred.append((due, dseq[0], fn))
                deferred.sort(key=lambda t: (t[0], t[1]))

            def run_deferred(now):
                while deferred and deferred[0][0] <= now:
                    deferred.pop(0)[2](now)

            def emit_S(idx):
                qb_, c_, kt_ = iters[idx]
                sk = (sbase + idx) % 3
                QTc = QT1 if c_ == 0 else QT2
                mm(psb[sk][:, :], KT[:, kt_ * 128:(kt_ + 1) * 128], QTc[:, qb_ * TB:(qb_ + 1) * TB], True, True,
                   [bK, bQ[qb_]], [pB[sk]])

            sbase = scnt
            for idx in range(min(LOOK, nit)):
                emit_S(idx)
            for idx, (qb, c, kt) in enumerate(iters):
                qs = slice(qb * TB, (qb + 1) * TB)
                run_deferred(idx)
                if kt == 0:
                    ob = 3 + (ucnt % 2)
                    sbk = 5 + (ucnt % 2)
                    ucnt += 1
                if idx + LOOK < nit:
                    emit_S(idx + LOOK)
                sk = (sbase + idx) % 3
                pk = (sbase + idx) % NPB
                act(pbuf[pk], psb[sk][:, :], AF.Exp, [pB[sk]], [bp[pk]], scale=8.0)
                mm(psb[ob][:, :], VA[:, kt, :], pbuf[pk], kt == 0, kt == 31, [bV, bp[pk]], [pB[ob]])
                if kt % 2 == 1:
                    qi = (idx // 2) % NQS
                    pk1 = (sbase + idx - 1) % NPB
                    tt(qs_[qi], pbuf[pk1], pbuf[pk], ALU.add, [bp[pk1], bp[pk]], [bqs[qi]],
                       eng=("pool" if (idx // 2) % 3 != 2 else "dve"))

                    def sum_part(now, qi=qi, sbk=sbk, kt=kt):
                        mm(psb[sbk][:, :], ones16, qs_[qi], kt == 1, kt == 31, [bconst, bqs[qi]], [pB[sbk]])

                    defer(idx + DSUM, sum_part)
                if kt == 31:
                    def epilogue(now, c=c, qb=qb, qs=qs, ob=ob, sbk=sbk, h=h):
                        recip(rinv, psb[sbk][:, :], [pB[sbk]], [brinv])
                        tt(oc[c], psb[ob][:, :], rinv, ALU.mult, [pB[ob], brinv], [boc2[c]])
                        if c == 1:
                            stt(oa, oc[1], lv[:, 16:17], oc[0], ALU.mult, ALU.add, [boc2[0], boc2[1], blv], [boa])
                            tt(sqb, oa, oa, ALU.mult, [boa], [bsqb])

                            def pe_part(now2):
                                mm(psb[7][:, :], ones16, sqb, True, True, [bsqb, bconst], [pB[7]])

                            def tail_part(now2):
                                act(lnb, psb[7][:, :], AF.Ln, [pB[7]], [blnb], bias=float(128 * EPS))
                                act(rsb, lnb, AF.Exp, [blnb], [brsb], scale=-0.5)
                                oi = (h * NB + qb) % 2
                                stt(o16[oi], oa, lv[:, 17:18], rsb, ALU.mult, ALU.mult, [boa, brsb, blv], [bo16[oi]])
                                P.dma("sp", "o16_%d" % oi, oc_d.ap()[h * 128:(h + 1) * 128, qs], o16[oi],
                                      [bo16[oi]], [boc[qb]])

                            defer(now + 14, pe_part)
                            defer(now + 24, tail_part)

                    defer(idx + DSUM + 1, epilogue)
            scnt += nit
            while deferred:
                run_deferred(deferred[-1][0])
            if dbg in ("qk", "att") and h == 0:
                break

    def phase_b2(l):
        P.barrier()
        AR.reset()
        base = PBASE + l * LW
        wl = win_d.ap()[l]
        QBz = [AR.take(S * 2, BF16) for _ in range(2)]
        KB = AR.take(S * 2, BF16)
        VB = AR.take(48 * 130 * 2, BF16).rearrange("p (t a d) -> p t a d", a=2, d=65)
        acc = [AR.take(S * 4, F32) for _ in range(2)]
        bQB, bKB, bVB = P.buf("QB"), P.buf("KB"), P.buf("VB")
        bacc = [P.buf("accB%d" % a) for a in range(2)]
        memset(VB[:, :, :, 64:65], 1.0, [bVB])
        memset(QBz[0], 0.0, [bQB])
        memset(QBz[1], 0.0, [bQB])
        wq, wk, wv = [AR.take(8 * 128 * 2, BF16).rearrange("p (k n) -> p k n", n=128) for _ in range(3)]
        bwq, bwk, bwv = P.buf("wqb"), P.buf("wkb"), P.buf("wvb")
        T = [AR.take(TB * 2, BF16), AR.take(TB * 4, F32), AR.take(TB * 2, BF16), AR.take(TB * 4, F32),
             AR.take(TB * 4, F32), AR.take(TB * 4, F32), AR.take(TB * 4, F32)]
        bT = [P.buf("Tb%d" % i) for i in range(7)]
        NS = 5
        pbuf = [AR.take(256 * 2, BF16) for _ in range(NS)]
        bp = [P.buf("pbb%d" % i) for i in range(NS)]
        pm = [AR.take(256 * 2, BF16) for _ in range(NS)]
        bpm = [P.buf("pmb%d" % i) for i in range(NS)]
        pSh = [pB[i] for i in range(NS)]
        rinv = AR.take(TB * 4, F32)
        brinv = P.buf("rinvb")
        o16 = [AR.take(TB * 2, BF16) for _ in range(2)]
        bo16 = [P.buf("ob16_%d" % i) for i in range(2)]
        selr = AR.take(128 * 4, F32)
        bsel = P.buf("selr")
        memset(selr[:, 0:64], 0.0, [bsel])
        memset(selr[64:65, 0:64], 1.0, [bsel])
        mask_mid = masks[:, 0:256]
        mask_e0 = masks[:, 256:384]
        mask_e1 = masks[:, 384:512]
        LOOK = 4

        def sview_ps(slot, n):
            return psb[slot][:, 0:n]

        for pr in range(2):
            for g, (window, dil) in enumerate(GROUPS):
                L = S // dil
                nq = L // 128
                c0 = 3072 + g * 256 + pr * 128
                P.dma("pool", "wqb", wq, wslab(wl, 0, 8, c0, 128), (), [bwq])
                P.dma("pool", "wkb", wk, wslab(wl, 0, 8, c0 + 768, 128), (), [bwk])
                P.dma("pool", "wvb", wv, wslab(wl, 0, 8, c0 + 1536, 128), (), [bwv])
                for b in range(NB):
                    m_cnt = TB // dil
                    u0 = (b * TB) // dil

                    def gview(dst, R, m_cnt=m_cnt, u0=u0, dil=dil, L=L):
                        if dil == 1:
                            return dst[R, u0:u0 + m_cnt]
                        return dst[R, :].rearrange("p (r u) -> p r u", u=L)[:, :, u0:u0 + m_cnt]

                    def sview(src, R, dil=dil):
                        if dil == 1:
                            return src[R, :]
                        return src[R, :].rearrange("p (m r) -> p r m", r=dil)

                    def wr_q(t1, rstd, reads, gview=gview, sview=sview):
                        for a in range(2):
                            R = slice(a * 64, (a + 1) * 64)
                            tt(gview(QBz[a], R), sview(t1, R), sview(rstd, R), ALU.mult, reads, [bQB])

                    def wr_k(t1, rstd, reads, gview=gview, sview=sview):
                        R = slice(0, 128)
                        tt(gview(KB, R), sview(t1, R), sview(rstd, R), ALU.mult, reads, [bKB])

                    qk_pipeline(l, wq, bwq, base + 82, b, 128, wr_q, T, bT, (7, 3, 4))
                    qk_pipeline(l, wk, bwk, base + 83, b, 128, wr_k, T, bT, (7, 3, 4))
                tiles = []
                for r in range(dil):
                    tiles.append((r, 0, "e0"))
                    for jj in range(1, nq):
                        tiles.append((r, 128 * jj - 64, "mid"))
                    tiles.append((r, L - 128, "e1"))
                for ti, (r, ustart, kind) in enumerate(tiles):
                    bk = 3 + (ti % 2)
                    cslot = (ti // 2) % 4
                    t0 = ustart * dil + r
                    for kc in range(8):
                        mm(psb[bk][:, cslot * 128:(cslot + 1) * 128],
                           hT[:, kc, t0:t0 + 127 * dil + 1:dil] if dil > 1 else hT[:, kc, t0:t0 + 128],
                           wv[:, kc, :], kc == 0, kc == 7, [bwv] + bHT, [pB[bk]])
                    cp(VB[:, ti, :, 0:64], psb[bk][:, cslot * 128:(cslot + 1) * 128].rearrange("p (a d) -> p a d", d=64),
                       [pB[bk]], [bVB])
                its = [(a, ti) for a in range(2) for ti in range(len(tiles))]

                def tile_geom(ti):
                    r, ustart, kind = tiles[ti]
                    if kind == "e0":
                        return r, ustart, kind, 0, 128, mask_e0
                    if kind == "e1":
                        return r, ustart, kind, L - 128, 128, mask_e1
                    return r, ustart, kind, ustart - 64, 256, mask_mid

                def emit_S(i):
                    a, ti = its[i]
                    r, ustart, kind, q0, nqc, msk = tile_geom(ti)
                    slot = i % NS
                    kcol = r * L + ustart
                    mm(sview_ps(slot, nqc), KB[:, kcol:kcol + 128], QBz[a][:, r * L + q0:r * L + q0 + nqc],
                       True, True, [bKB, bQB], [pSh[slot]])

                for i in range(min(LOOK, len(its))):
                    emit_S(i)
                for i, (a, ti) in enumerate(its):
                    r, ustart, kind, q0, nqc, msk = tile_geom(ti)
                    slot = i % NS
                    if i + LOOK < len(its):
                        emit_S(i + LOOK)
                    act(pbuf[slot][:, 0:nqc], sview_ps(slot, nqc), AF.Exp, [pSh[slot]], [bp[slot]], scale=8.0)
                    tt(pm[slot][:, 0:nqc], pbuf[slot][:, 0:nqc], msk, ALU.mult, [bp[slot], bconst], [bpm[slot]])
                    for ii in range(nqc // 128):
                        jq = (q0 + ii * 128) // 128
                        first = (kind == "e0") or (kind == "mid" and ii == 1)
                        last = not first
                        if kind == "e1":
                            first, last = False, True
                        gq = r * nq + jq
                        obk = 5 + ((gq // 4) % 2)
                        mm(psb[obk][0:65, (gq % 4) * 128:(gq % 4 + 1) * 128], VB[:, ti, a, :],
                           pm[slot][:, ii * 128:(ii + 1) * 128], first, last, [bVB, bpm[slot]], [pB[obk]])
                        if last and gq % 4 == 3:
                            g0 = (gq - 3) * 128
                            rr = g0 // L
                            uu = g0 % L
                            if dil == 1:
                                dst = acc[a][0:65, uu:uu + 512]
                                src = psb[obk][0:65, :]
                                if g == 0:
                                    cp(dst, src, [pB[obk]], [bacc[a]])
                                else:
                                    tt(dst, dst, src, ALU.add, [pB[obk], bacc[a]], [bacc[a]])
                            else:
                                nres = max(1, 512 // L)
                                seg = min(512, L)
                                for q in range(nres):
                                    r2 = rr + q
                                    u2 = uu if nres == 1 else 0
                                    dst = acc[a][0:65, :].rearrange("p (u r) -> p r u", r=dil)[:, r2, u2:u2 + seg]
                                    src = psb[obk][0:65, q * seg:(q + 1) * seg]
                                    if g == 0:
                                        cp(dst, src, [pB[obk]], [bacc[a]])
                                    else:
                                        tt(dst, dst, src, ALU.add, [pB[obk], bacc[a]], [bacc[a]])
            for a in range(2):
                j = 2 * pr + a
                for b in range(NB):
                    cs = slice(b * TB, (b + 1) * TB)
                    nbk = 7 if b % 2 == 0 else 3
                    mm(psb[nbk][0:64, :], selr[0:65, 0:64], acc[a][0:65, cs], True, True, [bacc[a], bsel], [pB[nbk]])
                    act(T[3][0:64, :], psb[nbk][0:64, :], AF.Ln, [pB[nbk]], [bT[3]])
                    act(rinv[0:64, :], T[3][0:64, :], AF.Exp, [bT[3]], [brinv], scale=-1.0)
                    oi = b % 2
                    tt(o16[oi][0:64, :], acc[a][0:64, cs], rinv[0:64, :], ALU.mult, [bacc[a], brinv], [bo16[oi]])
                    P.dma("sp", "ob16_%d" % oi, oc_d.ap()[1024 + j * 64:1024 + (j + 1) * 64, cs], o16[oi][0:64, :],
                          [bo16[oi]], [boc[b]])

    def convert_weights(l):
        wl = win_d.ap()[l]
        th = []

        def add(dst, src):
            th.append(lambda: P.dma("pool", "cvt", dst, src, (), ()))

        for f in range(16):
            add(wg16.ap()[:, f], wslab(wl, 0, 8, 5376 + f * 128, 128))
        for f in range(8):
            add(wa16.ap()[:, f], wslab(wa_d.ap()[l], 0, 8, f * 128, 128))
            add(wb16.ap()[:, f], wslab(wb_d.ap()[l], 0, 2, f * 128, 128))
        for hh in range(2):
            add(wo16.ap()[:, hh], wslab(wo_d.ap()[l], 0, 8, hh * 512, 512))
        for f in range(32):
            add(wu16.ap()[:, f], wslab(wu_d.ap()[l], 0, 8, f * 128, 128))
        for hh in range(2):
            for kg in range(4):
                add(wd16.ap()[:, hh, kg * 8:(kg + 1) * 8], wslab(wd_d.ap()[l], kg * 1024, 8, hh * 512, 512))
        if l + 1 < DEPTH:
            for cc in range(12):
                add(ada16.ap()[:, cc], wslab(adaw_d.ap()[l + 1], 0, 8, cc * 512, 512))
        return th

    def phase_cd(l):
        P.barrier()
        AR.reset()
        base = PBASE + l * LW
        last_layer = (l == DEPTH - 1)
        big = AR.take(32 * TB * 2, BF16)
        bbig = P.buf("big_cd")
        mT = big[:, 0:8 * TB].rearrange("p (k n) -> p k n", n=TB)
        bmT = bbig
        uT = big.rearrange("p (k n) -> p k n", n=TB)
        buT = bbig
        X = AR.take(10 * TB * 2, BF16)
        bX = P.buf("X_cd")
        ocb = X.rearrange("p (k n) -> p k n", n=TB)
        bocb = bX
        h2T = X[:, 0:8 * TB].rearrange("p (k n) -> p k n", n=TB)
        bh2 = bX
        junk = X[:, 8 * TB:8 * TB + D]
        bjunk = bX
        xt = [AR.take(D * 4, F32) for _ in range(4)]
        bxt = P.buf("xt_cd")
        NWS = 3
        WS = [AR.take(8 * 512 * 2, BF16) for _ in range(NWS)]
        bWS = [P.buf("ws%d" % i) for i in range(NWS)]
        sm = [AR.take(TB * 4, F32) for _ in range(4)]
        bsm = [P.buf("smt%d" % i) for i in range(6)]
        xnA = AR.take(D * 4, F32)
        bxnA = [P.buf("xnA_cd")]
        xnB = AR.take(D * 4, F32)
        sm += [xnB[:, 0:TB], xnB[:, TB:2 * TB]]
        bxnB = [bsm[4], bsm[5]]
        ss = AR.take(64, F32)
        bss = P.buf("ss_cd")
        def g4(t, f4):
            return t.ap()[:, f4 * 4:(f4 + 1) * 4].rearrange("p f k n -> p (f k) n")

        slabs = []
        for f4 in range(2):
            slabs += [(g4(wg16, f4), (32, 128)), (g4(wa16, f4), (32, 128)),
                      (wg16.ap()[:, 8 + f4 * 4:8 + (f4 + 1) * 4].rearrange("p f k n -> p (f k) n"), (32, 128)),
                      (g4(wb16, f4), (8, 128))]
        for hh in range(2):
            slabs.append((wo16.ap()[:, hh], (8, 512)))
        for f4 in range(8):
            slabs.append((g4(wu16, f4), (32, 128)))
        for hh in range(2):
            for kg in range(4):
                slabs.append((wd16.ap()[:, hh, kg * 8:(kg + 1) * 8], (8, 512)))
        NSL = len(slabs)
        issued = [0]
        views = {}

        def issue_upto(gmax):
            while issued[0] <= gmax and issued[0] < NSL * NB:
                gi = issued[0]
                src, (nk, n) = slabs[gi % NSL]
                i = gi % NWS
                w = WS[i][:, 0:nk * n].rearrange("p (k n) -> p k n", n=n)
                P.dma("sp", "ws%d" % i, w, src, (), [bWS[i]])
                views[gi] = (w, bWS[i])
                issued[0] += 1

        used = [0]

        def wnext(live=1):
            gi = used[0]
            used[0] += 1
            issue_upto(gi + NWS - live)
            return views.pop(gi)

        x_src = x_d if l == 0 else xs_d
        x_dst = out_d if last_layer else xs_d

        def norm_block(geff_col, shift_c0, dstT, bdst):
            for i in range(4):
                act(junk, xt[i], AF.Square, [bxt], [bjunk, bss], accum_out=ss[:, i:i + 1])
            act(ss[:, 4:8], ss[:, 0:4], AF.Ln, [bss], [bss], bias=float(D * EPS))
            act(ss[:, 8:12], ss[:, 4:8], AF.Exp, [bss], [bss], scale=-0.5)
            for i in range(4):
                xnb, bxn = (xnA, bxnA) if i % 2 == 0 else (xnB, bxnB)
                ts(xnb, xt[i], ss[:, 8 + i:9 + i], None, ALU.mult, None, [bxt, bss], bxn)
                for f2 in range(2):
                    bk = 4 + 2 * (i % 2) + f2
                    for ff in range(4):
                        f = f2 * 4 + ff
                        tr(psb[bk][:, ff * 128:(ff + 1) * 128], xnb[:, f * 128:(f + 1) * 128], bxn + [bconst], [pB[bk]])
                    for ff in range(4):
                        f = f2 * 4 + ff
                        act(dstT[:, f, i * 128:(i + 1) * 128], psb[bk][:, ff * 128:(ff + 1) * 128], AF.Identity,
                            [pB[bk], blv, bmod], [bdst], bias=modT[:, shift_c0 + f:shift_c0 + f + 1],
                            scale=lv[:, geff_col + f:geff_col + f + 1])

        def load_block_inputs(b, with_x=True, with_oc=True):
            cs_ = slice(b * TB, (b + 1) * TB)
            if with_oc:
                P.dma("pool", "ocb", ocb, oc_d.ap()[:, cs_].rearrange("(k p) n -> p k n", p=128), [boc[b]], [bocb])
            if with_x:
                for i in range(4):
                    r0 = b * TB + i * 128
                    P.dma("pool", "xt_cd", xt[i], x_src.ap()[r0:r0 + 128, :], [bxs[b]] if l > 0 else [], [bxt])

        load_block_inputs(0)
        for b in range(NB):
            cs = slice(b * TB, (b + 1) * TB)
            for f4 in range(2):
                wga, bwga = wnext()
                for ff in range(4):
                    f = f4 * 4 + ff
                    for kc in range(8):
                        mm(psb[ff][:, :], wga[:, ff * 8 + kc, :], hT[:, kc, cs], kc == 0, kc == 7, [bwga, bHT[b]], [pB[ff]])
                    act(sm[ff], psb[ff][:, :], AF.Sigmoid, [pB[ff], bconst], [bsm[ff]],
                        bias=par[:, base + 64 + f:base + 65 + f])
                wa_, bwa_ = wnext()
                for ff in range(4):
                    for kc in range(8):
                        mm(psb[4 + ff][:, :], wa_[:, ff * 8 + kc, :], ocb[:, kc, :], kc == 0, kc == 7, [bwa_, bocb], [pB[4 + ff]])
                    tt(sm[ff], sm[ff], psb[4 + ff][:, :], ALU.mult, [bsm[ff], pB[4 + ff]], [bsm[ff]])
                wgb, bwgb = wnext()
                wb_, bwb_ = wnext(live=2)
                for ff in range(4):
                    f = f4 * 4 + ff
                    s1 = 4 + (ff % 2)
                    for kc in range(8):
                        mm(psb[ff][:, :], wgb[:, ff * 8 + kc, :], hT[:, kc, cs], kc == 0, kc == 7, [bwgb, bHT[b]], [pB[ff]])
                    act(sm[s1], psb[ff][:, :], AF.Sigmoid, [pB[ff], bconst], [bsm[s1]],
                        bias=par[:, base + 72 + f:base + 73 + f])
                    for kc in range(2):
                        mm(psb[4 + ff][:, :], wb_[:, ff * 2 + kc, :], ocb[:, 8 + kc, :], kc == 0, kc == 1, [bwb_, bocb], [pB[4 + ff]])
                    tt(sm[s1], sm[s1], psb[4 + ff][:, :], ALU.mult, [bsm[s1], pB[4 + ff]], [bsm[s1]])
                    tt(mT[:, f, :], sm[ff], sm[s1], ALU.add, [bsm[ff], bsm[s1]], [bmT])
            for hh in range(2):
                wo_, bwo_ = wnext()
                for i in range(4):
                    bk = 4 * (i % 2) + 2 * hh
                    for kc in range(8):
                        mm(psb[bk][:, :], mT[:, kc, i * 128:(i + 1) * 128], wo_[:, kc, :], kc == 0, kc == 7,
                           [bwo_, bmT], [pB[bk]])
                    si = i % 2
                    tt(sm[si], psb[bk][:, :], gm_bc[:, hh * 512:(hh + 1) * 512], ALU.mult, [pB[bk], bgm], [bsm[si]])
                    tt(xt[i][:, hh * 512:(hh + 1) * 512], xt[i][:, hh * 512:(hh + 1) * 512], sm[si], ALU.add,
                       [bxt, bsm[si]], [bxt])
            norm_block(8, 24, h2T, bh2)
            for f4 in range(8):
                wu_, bwu_ = wnext()
                for ff in range(4):
                    f = f4 * 4 + ff
                    bk = ff % 4
                    for kc in range(8):
                        mm(psb[bk][:, :], wu_[:, ff * 8 + kc, :], h2T[:, kc, :], kc == 0, kc == 7,
                           [bwu_, bh2], [pB[bk]])
                    si = ff % 4
                    act(sm[si], psb[bk][:, :], AF.Relu, [pB[bk]], [bsm[si]])
                    tt(uT[:, f, :], sm[si], psb[bk][:, :], ALU.mult, [bsm[si], pB[bk]], [buT])
            if b + 1 < NB:
                load_block_inputs(b + 1, with_x=False)
            for hh in range(2):
                for kg in range(4):
                    wd_, bwd_ = wnext()
                    for i in range(4):
                        for kc in range(8):
                            k = kg * 8 + kc
                            mm(psb[4 + i][:, :], uT[:, k, i * 128:(i + 1) * 128], wd_[:, kc, :], k == 0, k == 31,
                               [bwd_, buT], [pB[4 + i]])
                for i in range(4):
                    si = i % 4
                    tt(sm[si], psb[4 + i][:, :], gf_bc[:, hh * 512:(hh + 1) * 512], ALU.mult, [pB[4 + i], bgf], [bsm[si]])
                    tt(xt[i][:, hh * 512:(hh + 1) * 512], xt[i][:, hh * 512:(hh + 1) * 512], sm[si], ALU.add,
                       [bxt, bsm[si]], [bxt])
            for i in range(4):
                r0 = b * TB + i * 128
                P.dma("pool", "xst", x_dst.ap()[r0:r0 + 128, :], xt[i], [bxt], [bxs[b]])
            if b + 1 < NB:
                load_block_inputs(b + 1, with_oc=False)

    def stage_a_from_scratch(l):
        P.barrier()
        AR.reset()
        xts = [[AR.take(D * 4, F32) for _ in range(4)] for _ in range(2)]
        bxts = [P.buf("xts%d" % i) for i in range(2)]
        xn = [AR.take(D * 4, F32) for _ in range(4)]
        bxn = [P.buf("xns%d" % i) for i in range(4)]
        junk = AR.take(D * 2, BF16)
        bjunk = P.buf("junks")
        ss = AR.take(64, F32)
        bss = P.buf("sss")
        for b in range(NB):
            xt, bxt = xts[b % 2], bxts[b % 2]
            for i in range(4):
                r0 = b * TB + i * 128
                P.dma("sp", "xts%d" % (b % 2), xt[i], xs_d.ap()[r0:r0 + 128, :], [bxs[b]], [bxt])
            stage_norm(xt, bxt, 0, lambda f: modT[:, f:f + 1],
                       lambda f, b=b: hT[:, f, b * TB:(b + 1) * TB], bHT[b],
                       xn, bxn, junk, bjunk, ss, bss, [0, 1, 2, 3])

    nlayers = DEPTH
    if dbg in ("rope",):
        nlayers = 0
    for l in range(nlayers):
        ada(l)
        if dbg == "ada":
            dump("d_modT", modT[:, :], [128, 48], F32, [bmod])
            dump("d_lv", lv[:, :], [128, 64], F32, [blv])
            dump("d_gm", gm_bc[:, :], [128, D], F32, [bgm])
            break
        if l == 0:
            stage_a0(l)
        else:
            stage_a_from_scratch(l)
        if dbg == "hT":
            dump("d_hT", hT[:, :, :], [128, 8, S], BF16, bHT)
            break
        phase_b1(l)
        if dbg in ("qk", "att"):
            P.barrier()
            dump("d_oc", oc_d.ap()[0:128, :], [128, S], BF16, boc)
            break
        phase_b2(l)
        if dbg == "b2":
            P.barrier()
            dump("d_oc", oc_d.ap()[:, :], [1280, S], BF16, boc)
            break
        phase_cd(l)
        if dbg == "l0":
            P.barrier()
            dump("d_xs", xs_d.ap(), [S, D], F32, bxs)
            break
    P.barrier()
    P.emit()
    return nc, list(dbg_out.keys())


def _consts():
    identf = np.eye(128, dtype=np.float32)
    cm = np.zeros((128, 384), np.float32)
    for blk in range(2):
        cm[blk * 64:(blk + 1) * 64, blk * 64:(blk + 1) * 64] = 1.0
    for blk in range(2):
        for i in range(8):
            a = blk * 64 + i
            cm[a + 8, 128 + a] = 1.0
            cm[a, 128 + a + 8] = 1.0
    cm[:, 256:384] = 1.0
    p = np.arange(128)[:, None]
    q = np.arange(256)[None, :]
    mask_mid = ((q >= p) & (q <= p + 128)).astype(np.float32)
    q1 = np.arange(128)[None, :]
    band = (np.abs(p - q1) <= 64)
    mask_e0 = (band & (p < 64)).astype(np.float32)
    mask_e1 = (band & (p >= 64)).astype(np.float32)
    masks = np.concatenate([mask_mid, mask_e0, mask_e1], axis=1)
    return identf, cm.astype(ml_dtypes.bfloat16), masks.astype(ml_dtypes.bfloat16)


def _params(inputs, b):
    par = np.zeros((128, NPAR), np.float32)
    par[:, 0:8] = inputs["c"][b].reshape(8, 128).T
    pm = np.arange(128) % 64
    rope = pm < 16
    f = (pm % 8).astype(np.float64)
    invf = (500000.0 ** (-(2.0 * f) / 16.0)).astype(np.float32)
    par[:, 8] = np.where(rope, invf, 0.0)
    par[:, 9] = np.where(rope, -1.0, 0.0)
    par[:, 10] = np.where(rope, 0.0, 1.0)
    par[:, 11] = np.where(pm < 8, -1.0, np.where(pm < 16, 1.0, 0.0))
    par[:, 9] = np.where(rope, 1.0, 0.0)
    for l in range(DEPTH):
        base = PBASE + l * LW
        par[:, base:base + 8] = inputs["norm_mix_g"][l].reshape(8, 128).T
        par[:, base + 8:base + 16] = inputs["norm_mlp_g"][l].reshape(8, 128).T
        par[:, base + 16:base + 64] = inputs["ada_b"][l].reshape(48, 128).T
        par[:, base + 64:base + 80] = inputs["gate_bias"][l].reshape(16, 128).T
        par[:, base + 80] = inputs["qk_gain_a"][l, 0][pm]
        par[:, base + 81] = inputs["qk_gain_a"][l, 1][pm]
        par[:, base + 82] = inputs["qk_gain_b"][l, 0][pm]
        par[:, base + 83] = inputs["qk_gain_b"][l, 1][pm]
        par[:, base + 84] = inputs["subln_g_a"][l]
        par[0:64, base + 85:base + 89] = inputs["lambda_a"][l].T
    return par


_CACHE = {}


def kernel(**inputs):
    dbg = os.environ.get("KDBG") or None
    ncores = int(os.environ.get("KCORES", "8"))
    inputs = {k: np.asarray(v) for k, v in inputs.items()}
    key = dbg
    if key not in _CACHE:
        _CACHE[key] = build(dbg)
    nc, dbg_names = _CACHE[key]
    identf, cm, masks = _consts()
    shared = {
        "identf": identf, "cm16": cm, "masks": masks,
        "ada_w": np.ascontiguousarray(inputs["ada_w"], np.float32),
        "w_in": np.ascontiguousarray(inputs["w_in"], np.float32),
        "w_branch_a": np.ascontiguousarray(inputs["w_branch_a"], np.float32),
        "w_branch_b": np.ascontiguousarray(inputs["w_branch_b"], np.float32),
        "w_out": np.ascontiguousarray(inputs["w_out"], np.float32),
        "w_mlp_up": np.ascontiguousarray(inputs["w_mlp_up"], np.float32),
        "w_mlp_down": np.ascontiguousarray(inputs["w_mlp_down"], np.float32),
    }
    in_maps = []
    for b in range(ncores):
        m = dict(shared)
        m["x"] = np.ascontiguousarray(inputs["x"][b], np.float32)
        m["pos"] = np.ascontiguousarray(inputs["positions"][b].reshape(1, S), np.int32)
        m["params"] = _params(inputs, b)
        in_maps.append(m)
    if dbg:
        return run_bass_kernel_spmd(nc, in_maps, core_ids=list(range(ncores)), trace=bool(os.environ.get('KTRACE')))
    res = run_bass_kernel_spmd(nc, in_maps, core_ids=list(range(ncores)))
    out = np.stack([np.asarray(r["out"], np.float32) for r in res.results], axis=0)
    return out
```

```python
import math
import os
import numpy as np
import ml_dtypes
import concourse.bass as bass
import concourse.mybir as mybir
from concourse.bass_utils import run_bass_kernel_spmd

F32 = mybir.dt.float32
BF16 = mybir.dt.bfloat16
I32 = mybir.dt.int32
AF = mybir.ActivationFunctionType
ALU = mybir.AluOpType

D = 1024
S = 4096
DEPTH = 2
DFF = 4096
INC = 7424
NB = 8
TB = 512
EPS = 1e-6
PI = math.pi
LW = 96
PBASE = 12
NPAR = PBASE + DEPTH * LW
GROUPS = ((128, 1), (512, 4), (2048, 16))


class Buf:
    __slots__ = ("name", "w", "r")

    def __init__(self, prog, name):
        self.name = name
        self.w = None
        self.r = []
        prog.bufs.append(self)


class Op:
    __slots__ = ("eng", "fn", "deps", "needs_inc", "tok", "is_dma")

    def __init__(self, eng, fn, is_dma=False):
        self.eng = eng
        self.fn = fn
        self.deps = []
        self.needs_inc = False
        self.tok = None
        self.is_dma = is_dma


ENGS = ("pe", "act", "dve", "pool", "sp")


class Prog:
    def __init__(self, nc):
        self.nc = nc
        self.ops = {e: [] for e in ENGS}
        self.bufs = []
        self.last = {e: None for e in ENGS}
        self.pending = {e: [] for e in ENGS}
        self.dma_keys = {}
        self.esem = {}

    def buf(self, name):
        return Buf(self, name)

    def _record(self, o, reads, writes):
        deps = []
        for b in reads:
            if b.w is not None:
                deps.append(b.w)
        for b in writes:
            if b.w is not None:
                deps.append(b.w)
            deps.extend(b.r)
        deps.extend(self.pending[o.eng])
        self.pending[o.eng] = []
        seen = set()
        for d in deps:
            if d is o or id(d) in seen:
                continue
            seen.add(id(d))
            if d.eng == "pe" and o.eng == "pe" and not d.is_dma and not o.is_dma:
                continue
            o.deps.append(d)
        for b in reads:
            b.r.append(o)
        for b in writes:
            b.w = o
            b.r = []
        self.ops[o.eng].append(o)
        if not o.is_dma:
            self.last[o.eng] = o
        return o

    def op(self, eng, fn, reads=(), writes=()):
        return self._record(Op(eng, fn), reads, writes)

    def dma(self, eng, key, out, in_, reads=(), writes=()):
        if key not in self.dma_keys:
            self.dma_keys[key] = [self.nc.semaphore("d_" + key).__enter__(), 0, None]
        ent = self.dma_keys[key]
        ent[1] += 1
        o = Op(eng, lambda e, out=out, in_=in_: e.dma_start(out=out, in_=in_), is_dma=True)
        o.tok = (ent[0], 16 * ent[1])
        self._record(o, reads, writes)
        o.deps = [d for d in o.deps if not (d.is_dma and d.tok[0] is ent[0] and d.eng == eng)]
        ent[2] = o
        return o

    def barrier(self):
        tails = [self.last[e] for e in ENGS if self.last[e] is not None]
        tails += [ent[2] for ent in self.dma_keys.values() if ent[2] is not None]
        for e in ENGS:
            self.pending[e] = list(tails)
        for b in self.bufs:
            b.w = None
            b.r = []

    def emit(self):
        nc = self.nc
        for e in ENGS:
            for o in self.ops[e]:
                for d in o.deps:
                    if not d.is_dma:
                        d.needs_inc = True
        for e in ENGS:
            if e not in self.esem:
                self.esem[e] = nc.semaphore("e_" + e).__enter__()
            cnt = 0
            for o in self.ops[e]:
                if not o.is_dma and o.needs_inc:
                    cnt += 1
                    o.tok = (self.esem[e], cnt)
        final_waits = [(ent[0], 16 * ent[1]) for ent in self.dma_keys.values()]

        def run(ename, eng):
            waited = {}
            for o in self.ops[ename]:
                need = {}
                for d in o.deps:
                    sem, val = d.tok
                    if need.get(id(sem), (None, 0))[1] < val:
                        need[id(sem)] = (sem, val)
                for sid, (sem, val) in need.items():
                    if waited.get(sid, 0) < val:
                        eng.wait_ge(sem, val)
                        waited[sid] = val
                ins = o.fn(eng)
                if o.is_dma:
                    ins.then_inc(o.tok[0], 16)
                elif o.needs_inc:
                    ins.then_inc(o.tok[0], 1)
            if ename == "sp":
                for sem, val in final_waits:
                    if waited.get(id(sem), 0) < val:
                        eng.wait_ge(sem, val)

        with nc.Block() as block:
            @block.tensor
            def _(eng):
                run("pe", eng)

            @block.scalar
            def _(eng):
                run("act", eng)

            @block.vector
            def _(eng):
                run("dve", eng)

            @block.gpsimd
            def _(eng):
                run("pool", eng)

            @block.sync
            def _(eng):
                run("sp", eng)


class Arena:
    def __init__(self, t16, nbytes):
        self.t16 = t16
        self.t32 = t16.bitcast(F32)
        self.ti32 = t16.bitcast(I32)
        self.nbytes = nbytes
        self.off = 0

    def reset(self):
        self.off = 0

    def take(self, nbytes, dtype):
        rb = (nbytes + 63) // 64 * 64
        o = self.off
        self.off += rb
        assert self.off <= self.nbytes, ("arena overflow", self.off, self.nbytes)
        if dtype == BF16:
            return self.t16[:, o // 2:(o + nbytes) // 2]
        if dtype == I32:
            return self.ti32[:, o // 4:(o + nbytes) // 4]
        return self.t32[:, o // 4:(o + nbytes) // 4]


def build(dbg=None):
    nc = bass.Bass("TRN2", target_bir_lowering=False)
    P = Prog(nc)
    dram = {}

    def din(name, shape, dt):
        dram[name] = nc.dram_tensor(name, list(shape), dt, kind="ExternalInput")
        return dram[name]

    x_d = din("x", [S, D], F32)
    pos_d = din("pos", [1, S], I32)
    par_d = din("params", [128, NPAR], F32)
    idf_d = din("identf", [128, 128], F32)
    cm_d = din("cm16", [128, 384], BF16)
    mk_d = din("masks", [128, 512], BF16)
    adaw_d = din("ada_w", [DEPTH, D, 6 * D], F32)
    win_d = din("w_in", [DEPTH, D, INC], F32)
    wa_d = din("w_branch_a", [DEPTH, D, D], F32)
    wb_d = din("w_branch_b", [DEPTH, 256, D], F32)
    wo_d = din("w_out", [DEPTH, D, D], F32)
    wu_d = din("w_mlp_up", [DEPTH, D, DFF], F32)
    wd_d = din("w_mlp_down", [DEPTH, DFF, D], F32)
    out_d = nc.dram_tensor("out", [S, D], F32, kind="ExternalOutput")
    xs_d = nc.dram_tensor("xs_scratch", [S, D], F32)
    oc_d = nc.dram_tensor("ocat_scratch", [1280, S], BF16)
    wg16 = nc.dram_tensor("wg16", [128, 16, 8, 128], BF16)
    ada16 = nc.dram_tensor("ada16", [128, 12, 8, 512], BF16)
    wa16 = nc.dram_tensor("wa16", [128, 8, 8, 128], BF16)
    wb16 = nc.dram_tensor("wb16", [128, 8, 2, 128], BF16)
    wo16 = nc.dram_tensor("wo16", [128, 2, 8, 512], BF16)
    wu16 = nc.dram_tensor("wu16", [128, 32, 8, 128], BF16)
    wd16 = nc.dram_tensor("wd16", [128, 2, 32, 512], BF16)
    dbg_out = {}

    def dbg_tensor(name, shape, dt):
        dbg_out[name] = nc.dram_tensor(name, list(shape), dt, kind="ExternalOutput")
        return dbg_out[name]

    sb = nc.alloc_sbuf_tensor
    hT = sb("hT", [128, 8, S], BF16)
    Ctab = sb("Ctab", [128, S], F32)
    Stab = sb("Stab", [128, S], F32)
    identf = sb("identf_sb", [128, 128], F32)
    onesf = sb("onesf", [128, 128], F32)
    cm16 = sb("cm16_sb", [128, 384], BF16)
    masks = sb("masks_sb", [128, 512], BF16)
    par = sb("par_sb", [128, NPAR], F32)
    cond16 = sb("cond16", [128, 8], BF16)
    modT = sb("modT", [128, 48], F32)
    lv = sb("lvec", [128, 64], F32)
    gm_bc = sb("gm_bc", [128, D], F32)
    gf_bc = sb("gf_bc", [128, D], F32)
    small = sb("small", [128, 64], F32)
    psb = [nc.alloc_psum_tensor("psb%d" % i, [128, 512], F32) for i in range(8)]
    pB = [P.buf("psum%d" % i) for i in range(8)]
    arena_bytes = (nc.sbuf_bytes_remaining - 1024) // 64 * 64
    arena_t = sb("arena", [128, arena_bytes // 2], BF16)
    AR = Arena(arena_t, arena_bytes)

    onesblk = cm16[:, 0:128]
    perm = cm16[:, 128:256]
    ones16 = cm16[:, 256:384]

    bHT = [P.buf("hT%d" % b) for b in range(NB)]
    bC = P.buf("Ctab")
    bS = P.buf("Stab")
    bconst = P.buf("consts")
    bmod = P.buf("modT")
    blv = P.buf("lv")
    bgm = P.buf("gm_bc")
    bgf = P.buf("gf_bc")
    bsmall = P.buf("small")
    bxs = [P.buf("xs%d" % b) for b in range(NB)]
    boc = [P.buf("oc%d" % b) for b in range(NB)]

    def mm(out, lhsT, rhs, start, stop, reads, writes):
        P.op("pe", lambda e: e.matmul(out, lhsT, rhs, start=start, stop=stop), reads, writes)

    def tr(out, in_, reads, writes):
        P.op("pe", lambda e: e.transpose(out, in_, identf[:, :]), reads, writes)

    def act(out, in_, func, reads, writes, bias=0.0, scale=1.0, accum_out=None):
        if accum_out is None:
            P.op("act", lambda e: e.activation(out, in_, func, bias=bias, scale=scale), reads, writes)
        else:
            P.op("act", lambda e: e.activation(out, in_, func, bias=bias, scale=scale,
                                                accum_out=accum_out), reads, writes)

    def ts(out, in0, s1, s2, op0, op1, reads, writes, eng="dve"):
        if s2 is None:
            P.op(eng, lambda e: e.tensor_scalar(out, in0, s1, None, op0), reads, writes)
        else:
            P.op(eng, lambda e: e.tensor_scalar(out, in0, s1, s2, op0, op1), reads, writes)

    def tt(out, in0, in1, op, reads, writes, eng="dve"):
        P.op(eng, lambda e: e.tensor_tensor(out, in0, in1, op), reads, writes)

    def stt(out, in0, scalar, in1, op0, op1, reads, writes):
        P.op("dve", lambda e: e.scalar_tensor_tensor(out, in0, scalar, in1, op0, op1), reads, writes)

    def cp(out, in_, reads, writes, eng="dve"):
        P.op(eng, lambda e: e.tensor_copy(out, in_), reads, writes)

    def recip(out, in_, reads, writes):
        P.op("dve", lambda e: e.reciprocal(out, in_), reads, writes)

    def memset(ap, val, writes, eng="dve"):
        P.op(eng, lambda e: e.memset(ap, val), (), writes)

    def wslab(l_w, rows0, nk, c0, nc_):
        return l_w[rows0:rows0 + nk * 128, c0:c0 + nc_].rearrange("(kc p) n -> p kc n", p=128)

    def dump(name, src_ap, shape, dt, reads):
        t = dbg_tensor(name, shape, dt)
        P.dma("sp", "dbg_" + name, t.ap(), src_ap, reads=reads, writes=())

    AR.reset()
    P.dma("sp", "c_par", par[:, :], par_d.ap(), (), [bconst])
    P.dma("sp", "c_idf", identf[:, :], idf_d.ap(), (), [bconst])
    P.dma("sp", "c_cm", cm16[:, :], cm_d.ap(), (), [bconst])
    P.dma("sp", "c_mk", masks[:, :], mk_d.ap(), (), [bconst])
    memset(onesf[:, :], 1.0, [bconst])
    act(cond16[:, :], par[:, 0:8], AF.Silu, [bconst], [bconst])

    invf = par[:, 8:9]
    a_c = par[:, 9:10]
    b_c = par[:, 10:11]
    a_s = par[:, 11:12]
    posi = [AR.take(TB * 4, I32) for _ in range(2)]
    bposi = [P.buf("posi%d" % i) for i in range(2)]
    rt = [AR.take(TB * 4, F32) for _ in range(6)]
    brt = [P.buf("rt%d" % i) for i in range(6)]
    C1 = 6.28125
    C2 = 2.0 * PI - C1
    def rope_block(b):
        cs = slice(b * TB, (b + 1) * TB)
        pi_, bpi = posi[b % 2], bposi[b % 2]
        src = bass.AP(pos_d, b * TB, [[0, 128], [1, TB]])
        P.dma("sp", "posi%d" % (b % 2), pi_, src, (), [bpi])
        ang, kf, r, m, rc, so = rt
        bang, bkf, br_, bm, brc, bso = brt
        cp(ang, pi_, [bpi], [bang])
        ts(ang, ang, invf, None, ALU.mult, None, [bang, bconst], [bang])
        ki = pi_
        ts(ki, ang, 1.0 / (2.0 * PI), None, ALU.mult, None, [bang], [bpi])
        cp(kf, ki, [bpi], [bkf])
        stt(r, kf, -C1, ang, ALU.mult, ALU.add, [bkf, bang], [br_])
        stt(r, kf, -C2, r, ALU.mult, ALU.add, [bkf, br_], [br_])
        ts(m, r, PI, 2.0 * PI, ALU.is_gt, ALU.mult, [br_], [bm])
        tt(r, r, m, ALU.subtract, [br_, bm], [br_])
        ts(m, r, -PI, 2.0 * PI, ALU.is_lt, ALU.mult, [br_], [bm])
        tt(r, r, m, ALU.add, [br_, bm], [br_])
        ts(rc, r, PI / 2.0, None, ALU.add, None, [br_], [brc])
        ts(m, rc, PI, 2.0 * PI, ALU.is_gt, ALU.mult, [brc], [bm])
        tt(rc, rc, m, ALU.subtract, [brc, bm], [brc])
        ts(r, r, PI, -PI, ALU.min, ALU.max, [br_], [br_])
        ts(rc, rc, PI, -PI, ALU.min, ALU.max, [brc], [brc])
        act(so, r, AF.Sin, [br_], [bso])
        ts(Stab[:, cs], so, a_s, None, ALU.mult, None, [bso, bconst], [bS])
        act(so, rc, AF.Sin, [brc], [bso])
        ts(Ctab[:, cs], so, a_c, b_c, ALU.mult, ALU.add, [bso, bconst], [bC])


    def ada(l):
        base = PBASE + l * LW
        if l == 0:
            wbuf = [AR.take(8 * 512 * 2, BF16).rearrange("p (k n) -> p k n", n=512) for _ in range(2)]
        else:
            P.barrier()
            AR.reset()
            wbuf = [AR.take(8 * 512 * 2, BF16).rearrange("p (k n) -> p k n", n=512) for _ in range(2)]
        bw = [P.buf("adaw%d_%d" % (l, i)) for i in range(2)]
        aw = adaw_d.ap()[l]
        for cc in range(12):
            w, bwb = wbuf[cc % 2], bw[cc % 2]
            if l == 0:
                P.dma("pool", "adaw%d" % (cc % 2), w, wslab(aw, 0, 8, cc * 512, 512), (), [bwb])
            else:
                P.dma("sp", "adaw%d" % (cc % 2), w, ada16.ap()[:, cc], (), [bwb])
            pb = pB[cc % 2]
            ps = psb[cc % 2]
            for f in range(4):
                for kc in range(8):
                    mm(ps[:, f:f + 1], w[:, kc, f * 128:(f + 1) * 128], cond16[:, kc:kc + 1],
                       kc == 0, kc == 7, [bwb, bconst], [pb])
            for f in range(4):
                col = cc * 4 + f
                act(modT[:, col:col + 1], ps[:, f:f + 1], AF.Identity, [pb, bconst], [bmod],
                    bias=par[:, base + 16 + col:base + 17 + col])
            if l == 0 and cc < NB:
                rope_block(cc)
        stt(lv[:, 0:8], modT[:, 8:16], 1.0, par[:, base:base + 8], ALU.add, ALU.mult, [bmod, bconst], [blv])
        ts(lv[:, 0:8], lv[:, 0:8], 32.0, None, ALU.mult, None, [blv], [blv])
        stt(lv[:, 8:16], modT[:, 32:40], 1.0, par[:, base + 8:base + 16], ALU.add, ALU.mult, [bmod, bconst], [blv])
        ts(lv[:, 8:16], lv[:, 8:16], 32.0, None, ALU.mult, None, [blv], [blv])
        G = [AR.take(128 * 4, F32) for _ in range(2)]
        bG = [P.buf("G%d" % i) for i in range(2)]
        for gi, (c0, dst, bdst) in enumerate(((16, gm_bc, bgm), (40, gf_bc, bgf))):
            for f in range(8):
                g, bg = G[f % 2], bG[f % 2]
                ts(g, onesf[:, :], modT[:, c0 + f:c0 + f + 1], None, ALU.mult, None, [bmod, bconst], [bg])
                pbk = pB[2 + (f // 4) + 2 * gi]
                tr(psb[2 + (f // 4) + 2 * gi][:, (f % 4) * 128:(f % 4 + 1) * 128], g, [bg, bconst], [pbk])
            for hh in range(2):
                cp(dst[:, hh * 512:(hh + 1) * 512], psb[2 + hh + 2 * gi][:, :], [pB[2 + hh + 2 * gi]], [bdst])
        lam_init = 0.8 - 0.6 * math.exp(-0.3 * l)
        lamb = base + 85
        tt(small[0:64, 0:1], par[0:64, lamb:lamb + 1], par[0:64, lamb + 1:lamb + 2], ALU.mult, [bconst], [bsmall])
        tt(small[0:64, 1:2], par[0:64, lamb + 2:lamb + 3], par[0:64, lamb + 3:lamb + 4], ALU.mult, [bconst, bsmall], [bsmall])
        mm(psb[6][:, 0:2], onesf[0:64, :], small[0:64, 0:2], True, True, [bsmall, bconst], [pB[6]])
        act(small[:, 2:4], psb[6][:, 0:2], AF.Exp, [pB[6]], [bsmall])
        tt(lv[:, 16:17], small[:, 3:4], small[:, 2:3], ALU.subtract, [bsmall], [blv])
        ts(lv[:, 16:17], lv[:, 16:17], -lam_init, None, ALU.add, None, [blv], [blv])
        ts(lv[:, 17:18], par[:, base + 84:base + 85], (1.0 - lam_init) * math.sqrt(128.0), None,
           ALU.mult, None, [bconst], [blv])

    def stage_norm(xt, bxt, geff_col, shift_ap, dstT, bdst, xn, bxn, junk, bjunk, ss, bss, pbanks):
        for i in range(4):
            act(junk, xt[i], AF.Square, [bxt], [bjunk, bss], accum_out=ss[:, i:i + 1])
        act(ss[:, 4:8], ss[:, 0:4], AF.Ln, [bss], [bss], bias=float(D * EPS))
        act(ss[:, 8:12], ss[:, 4:8], AF.Exp, [bss], [bss], scale=-0.5)
        for i in range(4):
            ts(xn[i], xt[i], ss[:, 8 + i:9 + i], None, ALU.mult, None, [bxt, bss], [bxn[i]])
        for f in range(8):
            bk = pbanks[f % len(pbanks)]
            for i in range(4):
                tr(psb[bk][:, i * 128:(i + 1) * 128], xn[i][:, f * 128:(f + 1) * 128], [bxn[i], bconst], [pB[bk]])
            act(dstT(f), psb[bk][:, :], AF.Identity, [pB[bk], blv, bmod], [bdst],
                bias=shift_ap(f), scale=lv[:, geff_col + f:geff_col + f + 1])

    def stage_a0(l):
        P.barrier()
        AR.reset()
        xts = [[AR.take(D * 4, F32) for _ in range(4)] for _ in range(2)]
        bxts = [P.buf("xt%d" % i) for i in range(2)]
        xn = [AR.take(D * 4, F32) for _ in range(4)]
        bxn = [P.buf("xn%d" % i) for i in range(4)]
        junk = AR.take(D * 2, BF16)
        bjunk = P.buf("junk")
        ss = AR.take(64, F32)
        bss = P.buf("ss")
        for b in range(NB):
            xt, bxt = xts[b % 2], bxts[b % 2]
            for i in range(4):
                r0 = b * TB + i * 128
                P.dma("sp", "xt%d" % (b % 2), xt[i], x_d.ap()[r0:r0 + 128, :], (), [bxt])
            stage_norm(xt, bxt, 0, lambda f: modT[:, f:f + 1],
                       lambda f, b=b: hT[:, f, b * TB:(b + 1) * TB], bHT[b],
                       xn, bxn, junk, bjunk, ss, bss, [0, 1, 2, 3])

    def qk_stages(l, wt, bwt, gain_col, b, nrows, writer, T, bT, banks):
        sq16, qg, qg16, lnv, rstd, t1, t2 = T
        bsq, bqg, bqg16, bln, brs, bt1, bt2 = bT
        pq, pss, psw = banks
        cs = slice(b * TB, (b + 1) * TB)
        R = slice(0, nrows)

        def stage1():
            for kc in range(8):
                mm(psb[pq][R, :], wt[:, kc, 0:nrows], hT[:, kc, cs], kc == 0, kc == 7, [bwt, bHT[b]], [pB[pq]])
            act(sq16[R, :], psb[pq][R, :], AF.Square, [pB[pq]], [bsq])
            act(qg[R, :], psb[pq][R, :], AF.Identity, [pB[pq], bconst], [bqg], scale=par[R, gain_col:gain_col + 1])
            cp(qg16[R, :], qg[R, :], [bqg], [bqg16])

        def stage2():
            mm(psb[pss][R, :], onesblk[R, 0:nrows], sq16[R, :], True, True, [bsq, bconst], [pB[pss]])
            mm(psb[psw][R, :], perm[R, 0:nrows], qg16[R, :], True, True, [bqg16, bconst], [pB[psw]])
            act(lnv[R, :], psb[pss][R, :], AF.Ln, [pB[pss]], [bln], bias=float(64 * EPS))
            act(rstd[R, :], lnv[R, :], AF.Exp, [bln], [brs], scale=-0.5)
            tt(t1[R, :], qg[R, :], Ctab[R, cs], ALU.mult, [bqg, bC], [bt1], eng="pool")
            tt(t2[R, :], psb[psw][R, :], Stab[R, cs], ALU.mult, [pB[psw], bS], [bt2])
            tt(t1[R, :], t1[R, :], t2[R, :], ALU.add, [bt1, bt2], [bt1])
            writer(t1, rstd, [bt1, brs])

        return stage1, stage2

    def qk_pipeline(l, wt, bwt, gain_col, b, nrows, writer, T, bT, banks):
        s1, s2 = qk_stages(l, wt, bwt, gain_col, b, nrows, writer, T, bT, banks)
        s1()
        s2()

    def run_pipelined(stage_pairs):
        if not stage_pairs:
            return
        stage_pairs[0][0]()
        for i, (_, s2) in enumerate(stage_pairs):
            if i + 1 < len(stage_pairs):
                stage_pairs[i + 1][0]()
            s2()

    def phase_b1(l):
        P.barrier()
        AR.reset()
        cvt = convert_weights(l)
        base = PBASE + l * LW
        wl = win_d.ap()[l]
        QT1 = AR.take(S * 2, BF16)
        QT2 = AR.take(S * 2, BF16)
        KT = AR.take(S * 2, BF16)
        VA = AR.take(S * 2, BF16).rearrange("p (t d) -> p t d", d=128)
        bQ = [P.buf("QT_%d" % b) for b in range(NB)]
        bK = P.buf("KT")
        bV = P.buf("VA")
        memset(QT1, 0.0, bQ)
        memset(QT2, 0.0, bQ)
        wq, wk, wv = [AR.take(8 * 128 * 2, BF16).rearrange("p (k n) -> p k n", n=128) for _ in range(3)]
        bwq, bwk, bwv = P.buf("wq"), P.buf("wk"), P.buf("wv")
        Ts = [[AR.take(TB * 2, BF16), AR.take(TB * 4, F32), AR.take(TB * 2, BF16), AR.take(TB * 4, F32),
               AR.take(TB * 4, F32), AR.take(TB * 4, F32), AR.take(TB * 4, F32)] for _ in range(2)]
        bTs = [[P.buf("T%d_%d" % (k, i)) for i in range(7)] for k in range(2)]
        NPB = 6
        pbuf = [AR.take(TB * 2, BF16) for _ in range(NPB)]
        NQS = 4
        qs_ = [AR.take(TB * 2, BF16) for _ in range(NQS)]
        bqs = [P.buf("qs%d" % i) for i in range(NQS)]
        bp = [P.buf("pbuf%d" % i) for i in range(NPB)]
        rinv = AR.take(TB * 4, F32)
        brinv = P.buf("rinv")
        oc = [AR.take(TB * 4, F32) for _ in range(2)]
        boc2 = [P.buf("ocx%d" % i) for i in range(2)]
        oa = AR.take(TB * 4, F32)
        boa = P.buf("oa")
        sqb = AR.take(TB * 2, BF16)
        bsqb = P.buf("sqb")
        lnb = AR.take(TB * 4, F32)
        blnb = P.buf("lnb")
        rsb = AR.take(TB * 4, F32)
        brsb = P.buf("rsb")
        o16 = [AR.take(TB * 2, BF16) for _ in range(2)]
        bo16 = [P.buf("o16_%d" % i) for i in range(2)]
        scnt = 0
        ucnt = 0
        for h in range(8):
            P.dma("pool", "wq", wq, wslab(wl, 0, 8, h * 128, 128), (), [bwq])
            P.dma("pool", "wk", wk, wslab(wl, 0, 8, 1024 + h * 128, 128), (), [bwk])
            P.dma("pool", "wv", wv, wslab(wl, 0, 8, 2048 + h * 128, 128), (), [bwv])
            ncv = (len(cvt) + 7 - h) // (8 - h) if dbg not in ("qk", "att") else len(cvt)
            for _ in range(ncv):
                cvt.pop(0)()
            stage_pairs = []
            for b in range(NB):
                cs = slice(b * TB, (b + 1) * TB)

                def wr_q(t1, rstd, reads, cs=cs, b=b):
                    tt(QT1[0:64, cs], t1[0:64, :], rstd[0:64, :], ALU.mult, reads, [bQ[b]])
                    tt(QT2[64:128, cs], t1[64:128, :], rstd[64:128, :], ALU.mult, reads, [bQ[b]])

                def wr_k(t1, rstd, reads, cs=cs):
                    tt(KT[:, cs], t1[:, :], rstd[:, :], ALU.mult, reads, [bK])

                stage_pairs.append(qk_stages(l, wq, bwq, base + 80, b, 128, wr_q, Ts[0], bTs[0], (7, 0, 1)))
                stage_pairs.append(qk_stages(l, wk, bwk, base + 81, b, 128, wr_k, Ts[1], bTs[1], (2, 3, 4)))
            run_pipelined(stage_pairs)
            for t4 in range(8):
                bk = 3 + (t4 % 2)
                for i in range(4):
                    tk = t4 * 4 + i
                    for kc in range(8):
                        mm(psb[bk][:, i * 128:(i + 1) * 128], hT[:, kc, tk * 128:(tk + 1) * 128], wv[:, kc, :],
                           kc == 0, kc == 7, [bwv, bHT[tk // 4]], [pB[bk]])
                cp(VA[:, t4 * 4:(t4 + 1) * 4, :], psb[bk][:, :].rearrange("p (t d) -> p t d", d=128),
                   [pB[bk]], [bV])
            if dbg in ("qk", "att") and h == 0 and l == 0:
                dump("d_QT1", QT1, [128, S], BF16, bQ)
                dump("d_QT2", QT2, [128, S], BF16, bQ)
                dump("d_KT", KT, [128, S], BF16, [bK])
                dump("d_VA", VA, [128, 32, 128], BF16, [bV])
            iters = [(qb, c, kt) for qb in range(NB) for c in range(2) for kt in range(32)]
            nit = len(iters)
            LOOK = 2
            DSUM = 4
            deferred = []
            dseq = [0]

            def defer(due, fn):
                dseq[0] += 1
                deferred.append((due, dseq[0], fn))
                deferred.sort(key=lambda t: (t[0], t[1]))

            def run_deferred(now):
                while deferred and deferred[0][0] <= now:
                    deferred.pop(0)[2](now)

            def emit_S(idx):
                qb_, c_, kt_ = iters[idx]
                sk = (sbase + idx) % 3
                QTc = QT1 if c_ == 0 else QT2
                mm(psb[sk][:, :], KT[:, kt_ * 128:(kt_ + 1) * 128], QTc[:, qb_ * TB:(qb_ + 1) * TB], True, True,
                   [bK, bQ[qb_]], [pB[sk]])

            sbase = scnt
            for idx in range(min(LOOK, nit)):
                emit_S(idx)
            for idx, (qb, c, kt) in enumerate(iters):
                qs = slice(qb * TB, (qb + 1) * TB)
                run_deferred(idx)
                if kt == 0:
                    ob = 3 + (ucnt % 2)
                    sbk = 5 + (ucnt % 2)
                    ucnt += 1
                if idx + LOOK < nit:
                    emit_S(idx + LOOK)
                sk = (sbase + idx) % 3
                pk = (sbase + idx) % NPB
                act(pbuf[pk], psb[sk][:, :], AF.Exp, [pB[sk]], [bp[pk]], scale=8.0)
                mm(psb[ob][:, :], VA[:, kt, :], pbuf[pk], kt == 0, kt == 31, [bV, bp[pk]], [pB[ob]])
                if kt % 2 == 1:
                    qi = (idx // 2) % NQS
                    pk1 = (sbase + idx - 1) % NPB
                    tt(qs_[qi], pbuf[pk1], pbuf[pk], ALU.add, [bp[pk1], bp[pk]], [bqs[qi]], eng="pool")

                    def sum_part(now, qi=qi, sbk=sbk, kt=kt):
                        mm(psb[sbk][:, :], ones16, qs_[qi], kt == 1, kt == 31, [bconst, bqs[qi]], [pB[sbk]])

                    defer(idx + DSUM, sum_part)
                if kt == 31:
                    def epilogue(now, c=c, qb=qb, qs=qs, ob=ob, sbk=sbk, h=h):
                        recip(rinv, psb[sbk][:, :], [pB[sbk]], [brinv])
                        tt(oc[c], psb[ob][:, :], rinv, ALU.mult, [pB[ob], brinv], [boc2[c]])
                        if c == 1:
                            stt(oa, oc[1], lv[:, 16:17], oc[0], ALU.mult, ALU.add, [boc2[0], boc2[1], blv], [boa])
                            tt(sqb, oa, oa, ALU.mult, [boa], [bsqb])

                            def pe_part(now2):
                                mm(psb[7][:, :], ones16, sqb, True, True, [bsqb, bconst], [pB[7]])

                            def tail_part(now2):
                                act(lnb, psb[7][:, :], AF.Ln, [pB[7]], [blnb], bias=float(128 * EPS))
                                act(rsb, lnb, AF.Exp, [blnb], [brsb], scale=-0.5)
                                oi = (h * NB + qb) % 2
                                stt(o16[oi], oa, lv[:, 17:18], rsb, ALU.mult, ALU.mult, [boa, brsb, blv], [bo16[oi]])
                                P.dma("sp", "o16_%d" % oi, oc_d.ap()[h * 128:(h + 1) * 128, qs], o16[oi],
                                      [bo16[oi]], [boc[qb]])

                            defer(now + 14, pe_part)
                            defer(now + 24, tail_part)

                    defer(idx + DSUM + 1, epilogue)
            scnt += nit
            while deferred:
                run_deferred(deferred[-1][0])
            if dbg in ("qk", "att") and h == 0:
                break

    def phase_b2(l):
        P.barrier()
        AR.reset()
        base = PBASE + l * LW
        wl = win_d.ap()[l]
        QBz = [AR.take(S * 2, BF16) for _ in range(2)]
        KB = AR.take(S * 2, BF16)
        VB = AR.take(48 * 130 * 2, BF16).rearrange("p (t a d) -> p t a d", a=2, d=65)
        acc = [AR.take(S * 4, F32) for _ in range(2)]
        bQB, bKB, bVB = P.buf("QB"), P.buf("KB"), P.buf("VB")
        bacc = [P.buf("accB%d" % a) for a in range(2)]
        memset(VB[:, :, :, 64:65], 1.0, [bVB])
        memset(QBz[0], 0.0, [bQB])
        memset(QBz[1], 0.0, [bQB])
        wq, wk, wv = [AR.take(8 * 128 * 2, BF16).rearrange("p (k n) -> p k n", n=128) for _ in range(3)]
        bwq, bwk, bwv = P.buf("wqb"), P.buf("wkb"), P.buf("wvb")
        T = [AR.take(TB * 2, BF16), AR.take(TB * 4, F32), AR.take(TB * 2, BF16), AR.take(TB * 4, F32),
             AR.take(TB * 4, F32), AR.take(TB * 4, F32), AR.take(TB * 4, F32)]
        bT = [P.buf("Tb%d" % i) for i in range(7)]
        NS = 5
        pbuf = [AR.take(256 * 2, BF16) for _ in range(NS)]
        bp = [P.buf("pbb%d" % i) for i in range(NS)]
        pm = [AR.take(256 * 2, BF16) for _ in range(NS)]
        bpm = [P.buf("pmb%d" % i) for i in range(NS)]
        pSh = [pB[i] for i in range(NS)]
        rinv = AR.take(TB * 4, F32)
        brinv = P.buf("rinvb")
        o16 = [AR.take(TB * 2, BF16) for _ in range(2)]
        bo16 = [P.buf("ob16_%d" % i) for i in range(2)]
        selr = AR.take(128 * 4, F32)
        bsel = P.buf("selr")
        memset(selr[:, 0:64], 0.0, [bsel])
        memset(selr[64:65, 0:64], 1.0, [bsel])
        mask_mid = masks[:, 0:256]
        mask_e0 = masks[:, 256:384]
        mask_e1 = masks[:, 384:512]
        LOOK = 4

        def sview_ps(slot, n):
            return psb[slot][:, 0:n]

        for pr in range(2):
            for g, (window, dil) in enumerate(GROUPS):
                L = S // dil
                nq = L // 128
                c0 = 3072 + g * 256 + pr * 128
                P.dma("pool", "wqb", wq, wslab(wl, 0, 8, c0, 128), (), [bwq])
                P.dma("pool", "wkb", wk, wslab(wl, 0, 8, c0 + 768, 128), (), [bwk])
                P.dma("pool", "wvb", wv, wslab(wl, 0, 8, c0 + 1536, 128), (), [bwv])
                for b in range(NB):
                    m_cnt = TB // dil
                    u0 = (b * TB) // dil

                    def gview(dst, R, m_cnt=m_cnt, u0=u0, dil=dil, L=L):
                        if dil == 1:
                            return dst[R, u0:u0 + m_cnt]
                        return dst[R, :].rearrange("p (r u) -> p r u", u=L)[:, :, u0:u0 + m_cnt]

                    def sview(src, R, dil=dil):
                        if dil == 1:
                            return src[R, :]
                        return src[R, :].rearrange("p (m r) -> p r m", r=dil)

                    def wr_q(t1, rstd, reads, gview=gview, sview=sview):
                        for a in range(2):
                            R = slice(a * 64, (a + 1) * 64)
                            tt(gview(QBz[a], R), sview(t1, R), sview(rstd, R), ALU.mult, reads, [bQB])

                    def wr_k(t1, rstd, reads, gview=gview, sview=sview):
                        R = slice(0, 128)
                        tt(gview(KB, R), sview(t1, R), sview(rstd, R), ALU.mult, reads, [bKB])

                    qk_pipeline(l, wq, bwq, base + 82, b, 128, wr_q, T, bT, (7, 3, 4))
                    qk_pipeline(l, wk, bwk, base + 83, b, 128, wr_k, T, bT, (7, 3, 4))
                tiles = []
                for r in range(dil):
                    tiles.append((r, 0, "e0"))
                    for jj in range(1, nq):
                        tiles.append((r, 128 * jj - 64, "mid"))
                    tiles.append((r, L - 128, "e1"))
                for ti, (r, ustart, kind) in enumerate(tiles):
                    bk = 3 + (ti % 2)
                    cslot = (ti // 2) % 4
                    t0 = ustart * dil + r
                    for kc in range(8):
                        mm(psb[bk][:, cslot * 128:(cslot + 1) * 128],
                           hT[:, kc, t0:t0 + 127 * dil + 1:dil] if dil > 1 else hT[:, kc, t0:t0 + 128],
                           wv[:, kc, :], kc == 0, kc == 7, [bwv] + bHT, [pB[bk]])
                    cp(VB[:, ti, :, 0:64], psb[bk][:, cslot * 128:(cslot + 1) * 128].rearrange("p (a d) -> p a d", d=64),
                       [pB[bk]], [bVB])
                its = [(a, ti) for a in range(2) for ti in range(len(tiles))]

                def tile_geom(ti):
                    r, ustart, kind = tiles[ti]
                    if kind == "e0":
                        return r, ustart, kind, 0, 128, mask_e0
                    if kind == "e1":
                        return r, ustart, kind, L - 128, 128, mask_e1
                    return r, ustart, kind, ustart - 64, 256, mask_mid

                def emit_S(i):
                    a, ti = its[i]
                    r, ustart, kind, q0, nqc, msk = tile_geom(ti)
                    slot = i % NS
                    kcol = r * L + ustart
                    mm(sview_ps(slot, nqc), KB[:, kcol:kcol + 128], QBz[a][:, r * L + q0:r * L + q0 + nqc],
                       True, True, [bKB, bQB], [pSh[slot]])

                for i in range(min(LOOK, len(its))):
                    emit_S(i)
                for i, (a, ti) in enumerate(its):
                    r, ustart, kind, q0, nqc, msk = tile_geom(ti)
                    slot = i % NS
                    if i + LOOK < len(its):
                        emit_S(i + LOOK)
                    act(pbuf[slot][:, 0:nqc], sview_ps(slot, nqc), AF.Exp, [pSh[slot]], [bp[slot]], scale=8.0)
                    tt(pm[slot][:, 0:nqc], pbuf[slot][:, 0:nqc], msk, ALU.mult, [bp[slot], bconst], [bpm[slot]])
                    for ii in range(nqc // 128):
                        jq = (q0 + ii * 128) // 128
                        first = (kind == "e0") or (kind == "mid" and ii == 1)
                        last = not first
                        if kind == "e1":
                            first, last = False, True
                        gq = r * nq + jq
                        obk = 5 + ((gq // 4) % 2)
                        mm(psb[obk][0:65, (gq % 4) * 128:(gq % 4 + 1) * 128], VB[:, ti, a, :],
                           pm[slot][:, ii * 128:(ii + 1) * 128], first, last, [bVB, bpm[slot]], [pB[obk]])
                        if last and gq % 4 == 3:
                            g0 = (gq - 3) * 128
                            rr = g0 // L
                            uu = g0 % L
                            if dil == 1:
                                dst = acc[a][0:65, uu:uu + 512]
                                src = psb[obk][0:65, :]
                                if g == 0:
                                    cp(dst, src, [pB[obk]], [bacc[a]])
                                else:
                                    tt(dst, dst, src, ALU.add, [pB[obk], bacc[a]], [bacc[a]])
                            else:
                                nres = max(1, 512 // L)
                                seg = min(512, L)
                                for q in range(nres):
                                    r2 = rr + q
                                    u2 = uu if nres == 1 else 0
                                    dst = acc[a][0:65, :].rearrange("p (u r) -> p r u", r=dil)[:, r2, u2:u2 + seg]
                                    src = psb[obk][0:65, q * seg:(q + 1) * seg]
                                    if g == 0:
                                        cp(dst, src, [pB[obk]], [bacc[a]])
                                    else:
                                        tt(dst, dst, src, ALU.add, [pB[obk], bacc[a]], [bacc[a]])
            for a in range(2):
                j = 2 * pr + a
                for b in range(NB):
                    cs = slice(b * TB, (b + 1) * TB)
                    nbk = 7 if b % 2 == 0 else 3
                    mm(psb[nbk][0:64, :], selr[0:65, 0:64], acc[a][0:65, cs], True, True, [bacc[a], bsel], [pB[nbk]])
                    act(T[3][0:64, :], psb[nbk][0:64, :], AF.Ln, [pB[nbk]], [bT[3]])
                    act(rinv[0:64, :], T[3][0:64, :], AF.Exp, [bT[3]], [brinv], scale=-1.0)
                    oi = b % 2
                    tt(o16[oi][0:64, :], acc[a][0:64, cs], rinv[0:64, :], ALU.mult, [bacc[a], brinv], [bo16[oi]])
                    P.dma("sp", "ob16_%d" % oi, oc_d.ap()[1024 + j * 64:1024 + (j + 1) * 64, cs], o16[oi][0:64, :],
                          [bo16[oi]], [boc[b]])

    def convert_weights(l):
        wl = win_d.ap()[l]
        th = []

        def add(dst, src):
            th.append(lambda: P.dma("pool", "cvt", dst, src, (), ()))

        for f in range(16):
            add(wg16.ap()[:, f], wslab(wl, 0, 8, 5376 + f * 128, 128))
        for f in range(8):
            add(wa16.ap()[:, f], wslab(wa_d.ap()[l], 0, 8, f * 128, 128))
            add(wb16.ap()[:, f], wslab(wb_d.ap()[l], 0, 2, f * 128, 128))
        for hh in range(2):
            add(wo16.ap()[:, hh], wslab(wo_d.ap()[l], 0, 8, hh * 512, 512))
        for f in range(32):
            add(wu16.ap()[:, f], wslab(wu_d.ap()[l], 0, 8, f * 128, 128))
        for hh in range(2):
            for kg in range(4):
                add(wd16.ap()[:, hh, kg * 8:(kg + 1) * 8], wslab(wd_d.ap()[l], kg * 1024, 8, hh * 512, 512))
        if l + 1 < DEPTH:
            for cc in range(12):
                add(ada16.ap()[:, cc], wslab(adaw_d.ap()[l + 1], 0, 8, cc * 512, 512))
        return th

    def phase_cd(l):
        P.barrier()
        AR.reset()
        base = PBASE + l * LW
        last_layer = (l == DEPTH - 1)
        big = AR.take(32 * TB * 2, BF16)
        bbig = P.buf("big_cd")
        mT = big[:, 0:8 * TB].rearrange("p (k n) -> p k n", n=TB)
        bmT = bbig
        uT = big.rearrange("p (k n) -> p k n", n=TB)
        buT = bbig
        X = AR.take(10 * TB * 2, BF16)
        bX = P.buf("X_cd")
        ocb = X.rearrange("p (k n) -> p k n", n=TB)
        bocb = bX
        h2T = X[:, 0:8 * TB].rearrange("p (k n) -> p k n", n=TB)
        bh2 = bX
        junk = X[:, 8 * TB:8 * TB + D]
        bjunk = bX
        xt = [AR.take(D * 4, F32) for _ in range(4)]
        bxt = P.buf("xt_cd")
        NWS = 3
        WS = [AR.take(8 * 512 * 2, BF16) for _ in range(NWS)]
        bWS = [P.buf("ws%d" % i) for i in range(NWS)]
        sm = [AR.take(TB * 4, F32) for _ in range(4)]
        bsm = [P.buf("smt%d" % i) for i in range(6)]
        xnA = AR.take(D * 4, F32)
        bxnA = [P.buf("xnA_cd")]
        xnB = AR.take(D * 4, F32)
        sm += [xnB[:, 0:TB], xnB[:, TB:2 * TB]]
        bxnB = [bsm[4], bsm[5]]
        ss = AR.take(64, F32)
        bss = P.buf("ss_cd")
        def g4(t, f4):
            return t.ap()[:, f4 * 4:(f4 + 1) * 4].rearrange("p f k n -> p (f k) n")

        slabs = []
        for f4 in range(2):
            slabs += [(g4(wg16, f4), (32, 128)), (g4(wa16, f4), (32, 128)),
                      (wg16.ap()[:, 8 + f4 * 4:8 + (f4 + 1) * 4].rearrange("p f k n -> p (f k) n"), (32, 128)),
                      (g4(wb16, f4), (8, 128))]
        for hh in range(2):
            slabs.append((wo16.ap()[:, hh], (8, 512)))
        for f4 in range(8):
            slabs.append((g4(wu16, f4), (32, 128)))
        for hh in range(2):
            for kg in range(4):
                slabs.append((wd16.ap()[:, hh, kg * 8:(kg + 1) * 8], (8, 512)))
        NSL = len(slabs)
        issued = [0]
        views = {}

        def issue_upto(gmax):
            while issued[0] <= gmax and issued[0] < NSL * NB:
                gi = issued[0]
                src, (nk, n) = slabs[gi % NSL]
                i = gi % NWS
                w = WS[i][:, 0:nk * n].rearrange("p (k n) -> p k n", n=n)
                P.dma("sp", "ws%d" % i, w, src, (), [bWS[i]])
                views[gi] = (w, bWS[i])
                issued[0] += 1

        used = [0]

        def wnext(live=1):
            gi = used[0]
            used[0] += 1
            issue_upto(gi + NWS - live)
            return views.pop(gi)

        x_src = x_d if l == 0 else xs_d
        x_dst = out_d if last_layer else xs_d

        def norm_block(geff_col, shift_c0, dstT, bdst):
            for i in range(4):
                act(junk, xt[i], AF.Square, [bxt], [bjunk, bss], accum_out=ss[:, i:i + 1])
            act(ss[:, 4:8], ss[:, 0:4], AF.Ln, [bss], [bss], bias=float(D * EPS))
            act(ss[:, 8:12], ss[:, 4:8], AF.Exp, [bss], [bss], scale=-0.5)
            for i in range(4):
                xnb, bxn = (xnA, bxnA) if i % 2 == 0 else (xnB, bxnB)
                ts(xnb, xt[i], ss[:, 8 + i:9 + i], None, ALU.mult, None, [bxt, bss], bxn)
                for f2 in range(2):
                    bk = 4 + 2 * (i % 2) + f2
                    for ff in range(4):
                        f = f2 * 4 + ff
                        tr(psb[bk][:, ff * 128:(ff + 1) * 128], xnb[:, f * 128:(f + 1) * 128], bxn + [bconst], [pB[bk]])
                    for ff in range(4):
                        f = f2 * 4 + ff
                        act(dstT[:, f, i * 128:(i + 1) * 128], psb[bk][:, ff * 128:(ff + 1) * 128], AF.Identity,
                            [pB[bk], blv, bmod], [bdst], bias=modT[:, shift_c0 + f:shift_c0 + f + 1],
                            scale=lv[:, geff_col + f:geff_col + f + 1])

        def load_block_inputs(b, with_x=True, with_oc=True):
            cs_ = slice(b * TB, (b + 1) * TB)
            if with_oc:
                P.dma("pool", "ocb", ocb, oc_d.ap()[:, cs_].rearrange("(k p) n -> p k n", p=128), [boc[b]], [bocb])
            if with_x:
                for i in range(4):
                    r0 = b * TB + i * 128
                    P.dma("pool", "xt_cd", xt[i], x_src.ap()[r0:r0 + 128, :], [bxs[b]] if l > 0 else [], [bxt])

        load_block_inputs(0)
        for b in range(NB):
            cs = slice(b * TB, (b + 1) * TB)
            for f4 in range(2):
                wga, bwga = wnext()
                for ff in range(4):
                    f = f4 * 4 + ff
                    for kc in range(8):
                        mm(psb[ff][:, :], wga[:, ff * 8 + kc, :], hT[:, kc, cs], kc == 0, kc == 7, [bwga, bHT[b]], [pB[ff]])
                    act(sm[ff], psb[ff][:, :], AF.Sigmoid, [pB[ff], bconst], [bsm[ff]],
                        bias=par[:, base + 64 + f:base + 65 + f])
                wa_, bwa_ = wnext()
                for ff in range(4):
                    for kc in range(8):
                        mm(psb[4 + ff][:, :], wa_[:, ff * 8 + kc, :], ocb[:, kc, :], kc == 0, kc == 7, [bwa_, bocb], [pB[4 + ff]])
                    tt(sm[ff], sm[ff], psb[4 + ff][:, :], ALU.mult, [bsm[ff], pB[4 + ff]], [bsm[ff]])
                wgb, bwgb = wnext()
                wb_, bwb_ = wnext(live=2)
                for ff in range(4):
                    f = f4 * 4 + ff
                    s1 = 4 + (ff % 2)
                    for kc in range(8):
                        mm(psb[ff][:, :], wgb[:, ff * 8 + kc, :], hT[:, kc, cs], kc == 0, kc == 7, [bwgb, bHT[b]], [pB[ff]])
                    act(sm[s1], psb[ff][:, :], AF.Sigmoid, [pB[ff], bconst], [bsm[s1]],
                        bias=par[:, base + 72 + f:base + 73 + f])
                    for kc in range(2):
                        mm(psb[4 + ff][:, :], wb_[:, ff * 2 + kc, :], ocb[:, 8 + kc, :], kc == 0, kc == 1, [bwb_, bocb], [pB[4 + ff]])
                    tt(sm[s1], sm[s1], psb[4 + ff][:, :], ALU.mult, [bsm[s1], pB[4 + ff]], [bsm[s1]])
                    tt(mT[:, f, :], sm[ff], sm[s1], ALU.add, [bsm[ff], bsm[s1]], [bmT])
            for hh in range(2):
                wo_, bwo_ = wnext()
                for i in range(4):
                    bk = 4 * (i % 2) + 2 * hh
                    for kc in range(8):
                        mm(psb[bk][:, :], mT[:, kc, i * 128:(i + 1) * 128], wo_[:, kc, :], kc == 0, kc == 7,
                           [bwo_, bmT], [pB[bk]])
                    si = i % 2
                    tt(sm[si], psb[bk][:, :], gm_bc[:, hh * 512:(hh + 1) * 512], ALU.mult, [pB[bk], bgm], [bsm[si]])
                    tt(xt[i][:, hh * 512:(hh + 1) * 512], xt[i][:, hh * 512:(hh + 1) * 512], sm[si], ALU.add,
                       [bxt, bsm[si]], [bxt])
            norm_block(8, 24, h2T, bh2)
            for f4 in range(8):
                wu_, bwu_ = wnext()
                for ff in range(4):
                    f = f4 * 4 + ff
                    bk = ff % 4
                    for kc in range(8):
                        mm(psb[bk][:, :], wu_[:, ff * 8 + kc, :], h2T[:, kc, :], kc == 0, kc == 7,
                           [bwu_, bh2], [pB[bk]])
                    si = ff % 4
                    act(sm[si], psb[bk][:, :], AF.Relu, [pB[bk]], [bsm[si]])
                    tt(uT[:, f, :], sm[si], psb[bk][:, :], ALU.mult, [bsm[si], pB[bk]], [buT])
            if b + 1 < NB:
                load_block_inputs(b + 1, with_x=False)
            for hh in range(2):
                for kg in range(4):
                    wd_, bwd_ = wnext()
                    for i in range(4):
                        for kc in range(8):
                            k = kg * 8 + kc
                            mm(psb[4 + i][:, :], uT[:, k, i * 128:(i + 1) * 128], wd_[:, kc, :], k == 0, k == 31,
                               [bwd_, buT], [pB[4 + i]])
                for i in range(4):
                    si = i % 4
                    tt(sm[si], psb[4 + i][:, :], gf_bc[:, hh * 512:(hh + 1) * 512], ALU.mult, [pB[4 + i], bgf], [bsm[si]])
                    tt(xt[i][:, hh * 512:(hh + 1) * 512], xt[i][:, hh * 512:(hh + 1) * 512], sm[si], ALU.add,
                       [bxt, bsm[si]], [bxt])
            for i in range(4):
                r0 = b * TB + i * 128
                P.dma("pool", "xst", x_dst.ap()[r0:r0 + 128, :], xt[i], [bxt], [bxs[b]])
            if b + 1 < NB:
                load_block_inputs(b + 1, with_oc=False)

    def stage_a_from_scratch(l):
        P.barrier()
        AR.reset()
        xts = [[AR.take(D * 4, F32) for _ in range(4)] for _ in range(2)]
        bxts = [P.buf("xts%d" % i) for i in range(2)]
        xn = [AR.take(D * 4, F32) for _ in range(4)]
        bxn = [P.buf("xns%d" % i) for i in range(4)]
        junk = AR.take(D * 2, BF16)
        bjunk = P.buf("junks")
        ss = AR.take(64, F32)
        bss = P.buf("sss")
        for b in range(NB):
            xt, bxt = xts[b % 2], bxts[b % 2]
            for i in range(4):
                r0 = b * TB + i * 128
                P.dma("sp", "xts%d" % (b % 2), xt[i], xs_d.ap()[r0:r0 + 128, :], [bxs[b]], [bxt])
            stage_norm(xt, bxt, 0, lambda f: modT[:, f:f + 1],
                       lambda f, b=b: hT[:, f, b * TB:(b + 1) * TB], bHT[b],
                       xn, bxn, junk, bjunk, ss, bss, [0, 1, 2, 3])

    nlayers = DEPTH
    if dbg in ("rope",):
        nlayers = 0
    for l in range(nlayers):
        ada(l)
        if dbg == "ada":
            dump("d_modT", modT[:, :], [128, 48], F32, [bmod])
            dump("d_lv", lv[:, :], [128, 64], F32, [blv])
            dump("d_gm", gm_bc[:, :], [128, D], F32, [bgm])
            break
        if l == 0:
            stage_a0(l)
        else:
            stage_a_from_scratch(l)
        if dbg == "hT":
            dump("d_hT", hT[:, :, :], [128, 8, S], BF16, bHT)
            break
        phase_b1(l)
        if dbg in ("qk", "att"):
            P.barrier()
            dump("d_oc", oc_d.ap()[0:128, :], [128, S], BF16, boc)
            break
        phase_b2(l)
        if dbg == "b2":
            P.barrier()
            dump("d_oc", oc_d.ap()[:, :], [1280, S], BF16, boc)
            break
        phase_cd(l)
        if dbg == "l0":
            P.barrier()
            dump("d_xs", xs_d.ap(), [S, D], F32, bxs)
            break
    P.barrier()
    P.emit()
    return nc, list(dbg_out.keys())


def _consts():
    identf = np.eye(128, dtype=np.float32)
    cm = np.zeros((128, 384), np.float32)
    for blk in range(2):
        cm[blk * 64:(blk + 1) * 64, blk * 64:(blk + 1) * 64] = 1.0
    for blk in range(2):
        for i in range(8):
            a = blk * 64 + i
            cm[a + 8, 128 + a] = 1.0
            cm[a, 128 + a + 8] = 1.0
    cm[:, 256:384] = 1.0
    p = np.arange(128)[:, None]
    q = np.arange(256)[None, :]
    mask_mid = ((q >= p) & (q <= p + 128)).astype(np.float32)
    q1 = np.arange(128)[None, :]
    band = (np.abs(p - q1) <= 64)
    mask_e0 = (band & (p < 64)).astype(np.float32)
    mask_e1 = (band & (p >= 64)).astype(np.float32)
    masks = np.concatenate([mask_mid, mask_e0, mask_e1], axis=1)
    return identf, cm.astype(ml_dtypes.bfloat16), masks.astype(ml_dtypes.bfloat16)


def _params(inputs, b):
    par = np.zeros((128, NPAR), np.float32)
    par[:, 0:8] = inputs["c"][b].reshape(8, 128).T
    pm = np.arange(128) % 64
    rope = pm < 16
    f = (pm % 8).astype(np.float64)
    invf = (500000.0 ** (-(2.0 * f) / 16.0)).astype(np.float32)
    par[:, 8] = np.where(rope, invf, 0.0)
    par[:, 9] = np.where(rope, -1.0, 0.0)
    par[:, 10] = np.where(rope, 0.0, 1.0)
    par[:, 11] = np.where(pm < 8, -1.0, np.where(pm < 16, 1.0, 0.0))
    par[:, 9] = np.where(rope, 1.0, 0.0)
    for l in range(DEPTH):
        base = PBASE + l * LW
        par[:, base:base + 8] = inputs["norm_mix_g"][l].reshape(8, 128).T
        par[:, base + 8:base + 16] = inputs["norm_mlp_g"][l].reshape(8, 128).T
        par[:, base + 16:base + 64] = inputs["ada_b"][l].reshape(48, 128).T
        par[:, base + 64:base + 80] = inputs["gate_bias"][l].reshape(16, 128).T
        par[:, base + 80] = inputs["qk_gain_a"][l, 0][pm]
        par[:, base + 81] = inputs["qk_gain_a"][l, 1][pm]
        par[:, base + 82] = inputs["qk_gain_b"][l, 0][pm]
        par[:, base + 83] = inputs["qk_gain_b"][l, 1][pm]
        par[:, base + 84] = inputs["subln_g_a"][l]
        par[0:64, base + 85:base + 89] = inputs["lambda_a"][l].T
    return par


_CACHE = {}


def kernel(**inputs):
    dbg = os.environ.get("KDBG") or None
    ncores = int(os.environ.get("KCORES", "8"))
    inputs = {k: np.asarray(v) for k, v in inputs.items()}
    key = dbg
    if key not in _CACHE:
        _CACHE[key] = build(dbg)
    nc, dbg_names = _CACHE[key]
    identf, cm, masks = _consts()
    shared = {
        "identf": identf, "cm16": cm, "masks": masks,
        "ada_w": np.ascontiguousarray(inputs["ada_w"], np.float32),
        "w_in": np.ascontiguousarray(inputs["w_in"], np.float32),
        "w_branch_a": np.ascontiguousarray(inputs["w_branch_a"], np.float32),
        "w_branch_b": np.ascontiguousarray(inputs["w_branch_b"], np.float32),
        "w_out": np.ascontiguousarray(inputs["w_out"], np.float32),
        "w_mlp_up": np.ascontiguousarray(inputs["w_mlp_up"], np.float32),
        "w_mlp_down": np.ascontiguousarray(inputs["w_mlp_down"], np.float32),
    }
    in_maps = []
    for b in range(ncores):
        m = dict(shared)
        m["x"] = np.ascontiguousarray(inputs["x"][b], np.float32)
        m["pos"] = np.ascontiguousarray(inputs["positions"][b].reshape(1, S), np.int32)
        m["params"] = _params(inputs, b)
        in_maps.append(m)
    if dbg:
        return run_bass_kernel_spmd(nc, in_maps, core_ids=list(range(ncores)), trace=bool(os.environ.get('KTRACE')))
    res = run_bass_kernel_spmd(nc, in_maps, core_ids=list(range(ncores)))
    out = np.stack([np.asarray(r["out"], np.float32) for r in res.results], axis=0)
    return out
```
